# Optimizing a Trainium2 kernel written in Bass

```python
import math
import jax, jax.numpy as jnp
from jax import lax
import numpy as np

D_MODEL = 1024
BATCH = 16
SEQ = 4096
DEPTH = 4

NORM_EPS = 1e-6
N_BRANCHES = 4
BRANCH_WIDTH = 512
D_FF = 4 * D_MODEL
Q_BLOCK = 128

DN_HEADS = 4
DN_HEAD_DIM = 128
DN_CONV = 4
DN_CHUNK = 64

MLA_HEADS = 4
MLA_Q_RANK = 256
MLA_KV_RANK = 128
MLA_NOPE = 128
MLA_ROPE = 64
MLA_V = 128
MLA_QK_DIM = MLA_NOPE + MLA_ROPE
ROPE_THETA = 10000.0

SG_GROUPS = 4
SG_GROUP_DIM = 128
SG_CHUNK = 128

FOX_HEADS = 4
FOX_HEAD_DIM = 128

IN_SPLITS = (
    3 * DN_HEADS * DN_HEAD_DIM,
    DN_HEADS * DN_HEAD_DIM,
    DN_HEADS,
    DN_HEADS,
    MLA_Q_RANK,
    MLA_KV_RANK,
    MLA_ROPE,
    SG_GROUPS * SG_GROUP_DIM,
    SG_GROUPS * SG_GROUP_DIM,
    3 * FOX_HEADS * FOX_HEAD_DIM,
    FOX_HEADS,
    N_BRANCHES * D_MODEL,
)
D_IN = sum(IN_SPLITS)

kernel_name = "hybrid_gdn_mla_sgmlp_fox_trunk"


def rms_norm(x, g, eps=NORM_EPS):
    xf = x.astype(jnp.float32)
    y = xf * lax.rsqrt(jnp.mean(xf * xf, axis=-1, keepdims=True) + eps)
    return (y * g.astype(jnp.float32)).astype(x.dtype)


def l2_norm(x, eps=NORM_EPS):
    return x * lax.rsqrt(jnp.sum(x * x, axis=-1, keepdims=True) + eps)


def causal_depthwise_conv(x, w):
    K = w.shape[0]
    S = x.shape[1]
    xp = jnp.pad(x, ((0, 0), (K - 1, 0), (0, 0)))
    out = xp[:, 0:S, :] * w[0]
    for k in range(1, K):
        out = out + xp[:, k:k + S, :] * w[k]
    return out


def rope_tables(positions):
    inv_freq = ROPE_THETA ** (-jnp.arange(0, MLA_ROPE, 2, dtype=jnp.float32) / MLA_ROPE)
    ang = positions.astype(jnp.float32)[..., None] * inv_freq
    return jnp.cos(ang)[:, :, None, :], jnp.sin(ang)[:, :, None, :]


def apply_rope(x, cos, sin):
    xf = x.astype(jnp.float32)
    x1, x2 = jnp.split(xf, 2, axis=-1)
    return jnp.concatenate([x1 * cos - x2 * sin, x2 * cos + x1 * sin], axis=-1).astype(x.dtype)


def block_causal_attention(q, k, v, log_decay=None):
    B, H, S, dk = q.shape
    nb = S // Q_BLOCK
    scale = dk ** -0.5
    k_pos = jnp.arange(S)

    def to_blocks(a):
        return jnp.moveaxis(a.reshape(B, H, nb, Q_BLOCK, *a.shape[3:]), 2, 0)

    xs = {"idx": jnp.arange(nb), "q": to_blocks(q)}
    if log_decay is not None:
        xs["d"] = to_blocks(log_decay)

    def attend(blk):
        q_pos = blk["idx"] * Q_BLOCK + jnp.arange(Q_BLOCK)
        s = jnp.einsum("bhqd,bhkd->bhqk", blk["q"], k).astype(jnp.float32) * scale
        if log_decay is not None:
            s = s + (blk["d"][..., :, None] - log_decay[..., None, :])
        s = jnp.where(k_pos[None, :] <= q_pos[:, None], s, -jnp.inf)
        p = jax.nn.softmax(s, axis=-1).astype(v.dtype)
        return jnp.einsum("bhqk,bhkd->bhqd", p, v)

    out = lax.map(attend, xs)
    return jnp.moveaxis(out, 0, 2).reshape(B, H, S, v.shape[-1])


def gated_deltanet(qkv, z, a_logit, b_logit, conv_w, a_log, dt_bias, out_g):
    B, S, _ = qkv.shape
    H, dk, C = DN_HEADS, DN_HEAD_DIM, DN_CHUNK
    N = S // C
    qkv = jax.nn.silu(causal_depthwise_conv(qkv, conv_w)).astype(jnp.float32)
    q, k, v = jnp.split(qkv, 3, axis=-1)

    def heads(t):
        return t.reshape(B, S, H, dk).transpose(0, 2, 1, 3)

    q = l2_norm(heads(q)) * dk ** -0.5
    k = l2_norm(heads(k))
    v = heads(v)
    beta = jax.nn.sigmoid(b_logit.astype(jnp.float32)).transpose(0, 2, 1)
    g = (-jnp.exp(a_log.astype(jnp.float32))
         * jax.nn.softplus(a_logit.astype(jnp.float32) + dt_bias.astype(jnp.float32))).transpose(0, 2, 1)

    def chunk(t):
        return t.reshape(B, H, N, C, *t.shape[3:])

    q, k, v, beta, g = chunk(q), chunk(k), chunk(v), chunk(beta), chunk(g)
    gc = jnp.cumsum(g, axis=-1)
    tril = jnp.tril(jnp.ones((C, C), dtype=bool))
    strict = jnp.tril(jnp.ones((C, C), dtype=bool), -1)
    diff = gc[..., :, None] - gc[..., None, :]
    decay = jnp.where(tril, jnp.exp(jnp.where(tril, diff, 0.0)), 0.0)

    kb = k * beta[..., None]
    lower = jnp.where(strict, jnp.einsum("bhnid,bhnjd->bhnij", kb, k) * decay, 0.0)
    rhs = jnp.concatenate([v * beta[..., None], kb * jnp.exp(gc)[..., None]], axis=-1)
    sol = lax.linalg.triangular_solve(lower + jnp.eye(C, dtype=jnp.float32), rhs,
                                      left_side=True, lower=True, unit_diagonal=True)
    u, w = jnp.split(sol, 2, axis=-1)

    a_qk = jnp.einsum("bhnid,bhnjd->bhnij", q, k) * decay
    q_dec = q * jnp.exp(gc)[..., None]
    k_dec = k * jnp.exp(gc[..., -1:] - gc)[..., None]
    g_last = jnp.exp(gc[..., -1])

    def step(state, xs):
        q_i, k_i, u_i, w_i, a_i, gl_i = xs
        v_new = u_i - jnp.einsum("bhcd,bhde->bhce", w_i, state)
        o_i = jnp.einsum("bhcd,bhde->bhce", q_i, state) + jnp.einsum("bhij,bhje->bhie", a_i, v_new)
        state = state * gl_i[..., None, None] + jnp.einsum("bhcd,bhce->bhde", k_i, v_new)
        return state, o_i

    xs = tuple(jnp.moveaxis(t, 2, 0) for t in (q_dec, k_dec, u, w, a_qk, g_last))
    _, o = lax.scan(step, jnp.zeros((B, H, dk, dk), jnp.float32), xs)
    o = jnp.moveaxis(o, 0, 2).reshape(B, H, S, dk)
    o = rms_norm(o, out_g).transpose(0, 2, 1, 3).reshape(B, S, H * dk)
    return (o * jax.nn.silu(z.astype(jnp.float32))).astype(z.dtype)


def latent_attention(c_q, c_kv, k_rope, positions, q_norm_g, kv_norm_g, w_uq, w_ukv, qk_q_g, qk_k_g):
    B, S, _ = c_q.shape
    H = MLA_HEADS
    q = (rms_norm(c_q, q_norm_g) @ w_uq).reshape(B, S, H, MLA_QK_DIM)
    kv = (rms_norm(c_kv, kv_norm_g) @ w_ukv).reshape(B, S, H, MLA_NOPE + MLA_V)
    k_nope, v = jnp.split(kv, [MLA_NOPE], axis=-1)
    k = jnp.concatenate([k_nope, jnp.broadcast_to(k_rope[:, :, None, :], (B, S, H, MLA_ROPE))], axis=-1)
    q = rms_norm(q, qk_q_g)
    k = rms_norm(k, qk_k_g)
    cos, sin = rope_tables(positions)
    q = jnp.concatenate([q[..., :MLA_NOPE], apply_rope(q[..., MLA_NOPE:], cos, sin)], axis=-1)
    k = jnp.concatenate([k[..., :MLA_NOPE], apply_rope(k[..., MLA_NOPE:], cos, sin)], axis=-1)
    o = block_causal_attention(q.transpose(0, 2, 1, 3), k.transpose(0, 2, 1, 3), v.transpose(0, 2, 1, 3))
    return o.transpose(0, 2, 1, 3).reshape(B, S, H * MLA_V)


def spatial_gating(u, v, norm_g, w_s, b_s):
    B, S, _ = u.shape
    G, Cg, T = SG_GROUPS, SG_GROUP_DIM, SG_CHUNK
    N = S // T
    u = jax.nn.gelu(u)
    v = rms_norm(jax.nn.gelu(v).reshape(B, S, G, Cg), norm_g).reshape(B, N, T, G, Cg)
    w_causal = jnp.where(jnp.tril(jnp.ones((T, T), dtype=bool)), w_s, 0.0)
    mixed = jnp.einsum("gts,bnsgc->bntgc", w_causal, v) + b_s.T[None, None, :, :, None]
    return u * mixed.reshape(B, S, G * Cg)


def forgetting_attention(qkv, f_logit, f_bias, q_g, k_g):
    B, S, _ = qkv.shape
    H, dh = FOX_HEADS, FOX_HEAD_DIM
    q, k, v = jnp.split(qkv, 3, axis=-1)

    def heads(t):
        return t.reshape(B, S, H, dh).transpose(0, 2, 1, 3)

    q = rms_norm(heads(q), q_g)
    k = rms_norm(heads(k), k_g)
    log_f = jax.nn.log_sigmoid(f_logit.astype(jnp.float32) + f_bias.astype(jnp.float32))
    cum = jnp.cumsum(log_f, axis=1).transpose(0, 2, 1)
    o = block_causal_attention(q, k, heads(v), cum)
    return o.transpose(0, 2, 1, 3).reshape(B, S, H * dh)


def setup_inputs(seed: int = 0) -> dict:
    key = jax.random.key(seed)
    ks = jax.random.split(key, 26)
    L = DEPTH
    f32 = jnp.float32

    def nrm(k, shape, fan_in):
        return jax.random.normal(k, shape, f32) * fan_in ** -0.5

    def gain(k, shape):
        return 1.0 + 0.1 * jax.random.normal(k, shape, f32)

    dt = jnp.exp(jax.random.uniform(ks[4], (L, DN_HEADS), f32, math.log(1e-3), math.log(1e-1)))
    return {
        "x": jax.random.normal(ks[0], (BATCH, SEQ, D_MODEL), f32),
        "positions": jnp.broadcast_to(jnp.arange(SEQ, dtype=jnp.int32)[None, :], (BATCH, SEQ)),
        "norm1_g": gain(ks[1], (L, D_MODEL)),
        "w_in": nrm(ks[2], (L, D_MODEL, D_IN), D_MODEL),
        "dn_conv_w": nrm(ks[3], (L, DN_CONV, 3 * DN_HEADS * DN_HEAD_DIM), DN_CONV),
        "dn_a_log": jnp.log(jax.random.uniform(ks[5], (L, DN_HEADS), f32, 1.0, 16.0)),
        "dn_dt_bias": dt + jnp.log(-jnp.expm1(-dt)),
        "dn_out_norm_g": gain(ks[6], (L, DN_HEAD_DIM)),
        "mla_q_norm_g": gain(ks[7], (L, MLA_Q_RANK)),
        "mla_kv_norm_g": gain(ks[8], (L, MLA_KV_RANK)),
        "mla_w_uq": nrm(ks[9], (L, MLA_Q_RANK, MLA_HEADS * MLA_QK_DIM), MLA_Q_RANK),
        "mla_w_ukv": nrm(ks[10], (L, MLA_KV_RANK, MLA_HEADS * (MLA_NOPE + MLA_V)), MLA_KV_RANK),
        "mla_qk_q_g": gain(ks[11], (L, MLA_QK_DIM)),
        "mla_qk_k_g": gain(ks[12], (L, MLA_QK_DIM)),
        "sg_v_norm_g": gain(ks[13], (L, SG_GROUPS, SG_GROUP_DIM)),
        "sg_w_s": nrm(ks[14], (L, SG_GROUPS, SG_CHUNK, SG_CHUNK), SG_CHUNK),
        "sg_b_s": gain(ks[15], (L, SG_GROUPS, SG_CHUNK)),
        "fox_q_norm_g": gain(ks[16], (L, FOX_HEAD_DIM)),
        "fox_k_norm_g": gain(ks[17], (L, FOX_HEAD_DIM)),
        "fox_f_bias": 4.0 + jax.random.normal(ks[18], (L, FOX_HEADS), f32),
        "w_branch": nrm(ks[19], (L, N_BRANCHES, BRANCH_WIDTH, D_MODEL), BRANCH_WIDTH),
        "w_out": nrm(ks[20], (L, D_MODEL, D_MODEL), D_MODEL),
        "norm2_g": gain(ks[21], (L, D_MODEL)),
        "w_ff1": nrm(ks[22], (L, D_MODEL, D_FF), D_MODEL),
        "w_ff2": nrm(ks[23], (L, D_FF, D_MODEL), D_FF),
    }


def reference(x, positions, norm1_g, w_in, dn_conv_w, dn_a_log, dn_dt_bias, dn_out_norm_g,
              mla_q_norm_g, mla_kv_norm_g, mla_w_uq, mla_w_ukv, mla_qk_q_g, mla_qk_k_g,
              sg_v_norm_g, sg_w_s, sg_b_s, fox_q_norm_g, fox_k_norm_g, fox_f_bias,
              w_branch, w_out, norm2_g, w_ff1, w_ff2):
    B, S, _ = x.shape
    split_points = np.cumsum(IN_SPLITS)[:-1]
    for l in range(DEPTH):
        h = rms_norm(x, norm1_g[l])
        proj = h @ w_in[l]
        (dn_qkv, dn_z, dn_a, dn_b, mla_cq, mla_ckv, mla_kr,
         sg_u, sg_v, fox_qkv, fox_f, gate_logits) = jnp.split(proj, split_points, axis=-1)

        o_a = gated_deltanet(dn_qkv, dn_z, dn_a, dn_b, dn_conv_w[l], dn_a_log[l],
                             dn_dt_bias[l], dn_out_norm_g[l])
        o_b = latent_attention(mla_cq, mla_ckv, mla_kr, positions, mla_q_norm_g[l], mla_kv_norm_g[l],
                               mla_w_uq[l], mla_w_ukv[l], mla_qk_q_g[l], mla_qk_k_g[l])
        o_c = spatial_gating(sg_u, sg_v, sg_v_norm_g[l], sg_w_s[l], sg_b_s[l])
        o_d = forgetting_attention(fox_qkv, fox_f, fox_f_bias[l], fox_q_norm_g[l], fox_k_norm_g[l])

        branches = jnp.stack([o_a.astype(x.dtype), o_b.astype(x.dtype),
                              o_c.astype(x.dtype), o_d.astype(x.dtype)], axis=2)
        projected = jnp.einsum("bsiw,iwd->bsid", branches, w_branch[l])
        gates = jax.nn.sigmoid(gate_logits.reshape(B, S, N_BRANCHES, D_MODEL))
        merged = jnp.sum(gates * projected, axis=2)
        x = x + merged @ w_out[l]

        h2 = rms_norm(x, norm2_g[l])
        x = x + jnp.square(jax.nn.relu(h2 @ w_ff1[l])) @ w_ff2[l]
    return x
```

```python
import math
import os
GSTOP = int(os.environ.get('GSTOP', '99'))
GVAR = int(os.environ.get('GVAR', '0'))
PHASES = os.environ.get('PHASES', 'pa,gdn,mla,sg,fox,mf').split(',')
DMAQ = {'pool': os.environ.get('STOREQ', 'pool')}
import numpy as np
import concourse.bass as bass
import concourse.mybir as mybir
from concourse.bass_utils import run_bass_kernel_spmd

F32 = mybir.dt.float32
BF16 = mybir.dt.bfloat16
I32 = mybir.dt.int32
AF = mybir.ActivationFunctionType
ALU = mybir.AluOpType

COMPUTE = ("pe", "act", "dve", "pool")
EPOCH = 30000
D = 1024
DIN = 9164
EPS = 1e-6


class Slot:
    __slots__ = ("w", "r", "name", "ap", "lo", "hi")

    def __init__(self, name="", ap=None, lo=0, hi=0):
        self.w = None
        self.r = []
        self.name = name
        self.ap = ap
        self.lo = lo
        self.hi = hi


class Rec:
    __slots__ = ("eng", "fn", "deps", "signal", "val", "is_dma", "sem")

    def __init__(self, eng, fn, deps, is_dma=False):
        self.eng = eng
        self.fn = fn
        self.deps = deps
        self.signal = False
        self.val = 0
        self.is_dma = is_dma
        self.sem = None


class Prog:
    def __init__(self, nc, arena_words=52992, ndma_sems=int(os.environ.get('NDMA', '6'))):
        self.nc = nc
        self.streams = {k: [] for k in ("pe", "act", "dve", "pool", "sp")}
        self.dma_pool = {q: [nc.alloc_semaphore("d_%s%d" % (q, i)) for i in range(ndma_sems)]
                         for q in ("sp", "pool", "act")}
        self.dma_last = {q: [None] * ndma_sems for q in self.dma_pool}
        self.dma_rr = {q: 0 for q in self.dma_pool}
        self.arena = nc.alloc_sbuf_tensor("arena", [128, arena_words], F32)
        self.arena_words = arena_words
        self.top = 0
        self.live = []
        self.phase_marks = []
        self.psum = []
        for i in range(8):
            t = nc.alloc_psum_tensor("ps%d" % i, [128, 512], F32)
            self.psum.append(Slot("ps%d" % i, t[:, :]))

    def pm(self, name):
        self.phase_marks.append((name, len(self.streams['pe'])))

    def mark(self):
        return self.top

    def release(self, mark):
        self.top = mark

    def alloc(self, words, name="", dtype=None):
        req = words
        words = (words + 7) // 8 * 8
        lo, hi = self.top, self.top + words
        assert hi <= self.arena_words, "arena overflow %s %d" % (name, hi)
        self.top = hi
        ap = self.arena[:, lo:lo + req]
        if dtype is BF16:
            ap = ap.bitcast(BF16)
        elif dtype is I32:
            ap = ap.bitcast(I32)
        s = Slot(name, ap, lo, hi)
        keep = []
        for o in self.live:
            if o.lo < hi and lo < o.hi:
                if o.w is not None:
                    s.r.append(o.w)
                s.r.extend(o.r)
                if o.lo < lo or o.hi > hi:
                    keep.append(o)
            else:
                keep.append(o)
        keep.append(s)
        self.live = keep
        return s

    def _deps(self, reads, writes):
        deps = []
        for s in reads:
            if s.w is not None:
                deps.append(s.w)
        for s in writes:
            if s.w is not None:
                deps.append(s.w)
            deps.extend(s.r)
        return deps

    def _update(self, rec, reads, writes):
        for s in writes:
            s.w = rec
            s.r = []
        for s in reads:
            if s.w is rec:
                continue
            if rec.is_dma:
                s.r.append(rec)
            else:
                s.r = [x for x in s.r if x.is_dma or x.eng != rec.eng]
                s.r.append(rec)

    def op(self, eng, fn, reads=(), writes=()):
        rec = Rec(eng, fn, self._deps(reads, writes))
        self._update(rec, reads, writes)
        self.streams[eng].append(rec)
        return rec

    def dma(self, q, out, in_, reads=(), writes=(), **kw):
        q = DMAQ.get(q, q)
        deps = self._deps(reads, writes)
        k = self.dma_rr[q]
        self.dma_rr[q] = (k + 1) % len(self.dma_pool[q])
        prev = self.dma_last[q][k]
        if prev is not None:
            deps.append(prev)
        rec = Rec(q, lambda e: e.dma_start(out=out, in_=in_, allow_slow_non_contiguous=True, **kw), deps, is_dma=True)
        rec.sem = self.dma_pool[q][k]
        rec.val = (prev.val if prev is not None else 0) + 16
        rec.signal = True
        self.dma_last[q][k] = rec
        self._update(rec, reads, writes)
        self.streams[q].append(rec)
        return rec

    def emit(self):
        nc = self.nc
        for q, st in self.streams.items():
            for rec in st:
                for d in rec.deps:
                    if not d.is_dma:
                        if d.eng == "pe" and rec.eng == "pe" and not rec.is_dma:
                            continue
                        d.signal = True
        finals = {q: [r for r in self.dma_last[q] if r is not None] for q in self.dma_pool}
        for q in COMPUTE:
            c = 0
            sem = None
            for rec in self.streams[q]:
                if rec.is_dma:
                    continue
                if rec.signal:
                    if sem is None or c >= EPOCH:
                        sem = nc.alloc_semaphore("s_%s_%d" % (q, len(self.streams[q]) + id(rec) % 100000))
                        c = 0
                    c += 1
                    rec.val = c
                    rec.sem = sem
        engs = {"pe": "tensor", "act": "scalar", "dve": "vector", "pool": "gpsimd", "sp": "sync"}
        with nc.Block() as block:
            def mk(q):
                def body(e):
                    waited = {}
                    for rec in self.streams[q]:
                        need = {}
                        for d in rec.deps:
                            if (not d.is_dma) and d.eng == "pe" and q == "pe" and not rec.is_dma:
                                continue
                            key = d.sem.num
                            if waited.get(key, 0) >= d.val:
                                continue
                            if key not in need or need[key][1] < d.val:
                                need[key] = (d.sem, d.val)
                        for key, (sem, val) in need.items():
                            e.wait_ge(sem, val)
                            waited[key] = val
                        ins = rec.fn(e)
                        if rec.is_dma:
                            ins.then_inc(rec.sem, 16)
                        elif rec.signal:
                            ins.then_inc(rec.sem, 1)
                    for r in finals.get(q, []):
                        if waited.get(r.sem.num, 0) < r.val:
                            e.wait_ge(r.sem, r.val)
                return body
            for q, nm in engs.items():
                getattr(block, nm)(mk(q))


C_ID, C_MI, C_MS, C_ONE, C_ROT, C_INVF, C_LVL, C_END = 0, 128, 256, 384, 512, 576, 584, 584 + 8 * 128


def make_consts():
    c = np.zeros((128, C_END), np.float32)
    j = np.arange(128)[:, None]
    i = np.arange(128)[None, :]
    c[:, C_ID:C_ID + 128] = (i == j)
    c[:, C_MI:C_MI + 128] = (i >= j)
    c[:, C_MS:C_MS + 128] = (i > j)
    c[:, C_ONE:C_ONE + 128] = 1.0
    R = np.zeros((64, 64), np.float32)
    for m in range(32):
        R[m + 32, m] = -1.0
        R[m, m + 32] = 1.0
    c[0:64, C_ROT:C_ROT + 64] = R
    invf = (10000.0 ** (-np.arange(0, 64, 2, dtype=np.float32) / 64)).astype(np.float32)
    c[0:32, C_INVF] = invf
    c[32:64, C_INVF] = invf
    p = np.arange(128)[:, None]
    f = np.arange(128)[None, :]
    for m in range(7):
        n = 1 << m
        mt = ((f // (2 * n)) == (p // (2 * n))) & ((f % (2 * n)) >= n) & ((p % (2 * n)) < n)
        c[:, C_LVL + m * 128:C_LVL + (m + 1) * 128] = mt
    c[:, C_LVL + 7 * 128:C_LVL + 8 * 128] = c[:, C_LVL:C_LVL + 128].T
    return c


def build(NSEQ, S, DEPTH, dbg=None):
    NT = S // 128
    NB = S // 512
    nc = bass.Bass("TRN2", target_bir_lowering=False)

    def din(name, shape, dt=F32):
        return nc.dram_tensor(name, list(shape), dt, kind="ExternalInput").ap()

    L = DEPTH
    x_in = din("x", [NSEQ, S, D])
    pos_in = din("positions", [NSEQ, S], I32)
    consts_in = din("consts", [128, C_END])
    W = {}
    for name, shape in [("norm1_g", [L, D]), ("w_in", [L, D, DIN]), ("dn_conv_w", [L, 4, 1536]),
                        ("dn_a_log", [L, 4]), ("dn_dt_bias", [L, 4]), ("dn_out_norm_g", [L, 128]),
                        ("mla_q_norm_g", [L, 256]), ("mla_kv_norm_g", [L, 128]),
                        ("mla_w_uq", [L, 256, 768]), ("mla_w_ukv", [L, 128, 1024]),
                        ("mla_qk_q_g", [L, 192]), ("mla_qk_k_g", [L, 192]),
                        ("sg_v_norm_g", [L, 4, 128]), ("sg_w_s", [L, 4, 128, 128]), ("sg_b_s", [L, 4, 128]),
                        ("fox_q_norm_g", [L, 128]), ("fox_k_norm_g", [L, 128]), ("fox_f_bias", [L, 4]),
                        ("w_branch", [L, 4, 512, D]), ("w_out", [L, D, D]), ("norm2_g", [L, D]),
                        ("w_ff1", [L, D, 4 * D]), ("w_ff2", [L, 4 * D, D])]:
        W[name] = din(name, shape)
    out = nc.dram_tensor("out", [NSEQ, S, D], F32, kind="ExternalOutput").ap()

    def scratch(name, shape, dt=BF16):
        return nc.dram_tensor(name, list(shape), dt, kind=("ExternalOutput" if dbg else "Internal")).ap()
    lvl = dbg if dbg else 99

    SC = {n: scratch(n, [r, S]) for n, r in
          [("dnq", 512), ("dnk", 512), ("dnv", 512), ("dnz", 512), ("cq", 256), ("ckv", 128), ("kr", 64),
           ("sgu", 512), ("fq", 512), ("fk", 512), ("gates", 4096), ("br", 2048), ("mq", 768), ("mk", 768)]}
    SC["gsm"] = scratch("gsm", [S, 12], F32)
    SC["sgv"] = scratch("sgv", [S, 512])
    SC["fv"] = scratch("fv", [S, 512])
    SC["mv"] = scratch("mv", [S, 512])
    DS = {n: [Slot("%s%d" % (n, t)) for t in range(NT)] for n in SC}
    XS = [[Slot("x%d_%d" % (s, t)) for t in range(NT)] for s in range(NSEQ)]

    def dsl(name, t0, t1=None):
        return DS[name][t0:(t1 if t1 is not None else t0 + 1)]

    P = Prog(nc)
    op, dma = P.op, P.dma
    PS = P.psum

    def psb(i):
        return PS[i].ap.bitcast(BF16)

    def mm(ps, out_ap, lhsT, rhs, start, stop, reads):
        op("pe", lambda e: e.matmul(out_ap, lhsT=lhsT, rhs=rhs, start=start, stop=stop), reads=reads, writes=[ps])

    def tr(ps, out_ap, in_ap, ident, reads):
        op("pe", lambda e: e.transpose(out=out_ap, in_=in_ap, identity=ident), reads=reads, writes=[ps])

    def act(out_ap, in_ap, func, reads, writes, bias=None, scale=None, accum=None, eng="act"):
        kw = {}
        if bias is not None:
            kw["bias"] = bias
        if scale is not None:
            kw["scale"] = scale
        if accum is not None:
            kw["accum_out"] = accum
        op(eng, lambda e: e.activation(out=out_ap, in_=in_ap, func=func, **kw), reads=reads, writes=writes)

    def tt(out_ap, a, b, alu, reads, writes, eng="dve"):
        op(eng, lambda e: e.tensor_tensor(out=out_ap, in0=a, in1=b, op=alu), reads=reads, writes=writes)

    def ts(out_ap, a, s1, alu, reads, writes, s2=None, alu2=None, eng="dve"):
        if alu2 is None:
            op(eng, lambda e: e.tensor_scalar(out=out_ap, in0=a, scalar1=s1, scalar2=None, op0=alu), reads=reads, writes=writes)
        else:
            op(eng, lambda e: e.tensor_scalar(out=out_ap, in0=a, scalar1=s1, scalar2=s2, op0=alu, op1=alu2), reads=reads, writes=writes)

    def stt(out_ap, a, s, b, alu0, alu1, reads, writes, eng="dve"):
        op(eng, lambda e: e.scalar_tensor_tensor(out=out_ap, in0=a, scalar=s, in1=b, op0=alu0, op1=alu1), reads=reads, writes=writes)

    def cp(out_ap, in_ap, reads, writes, eng="dve"):
        if eng == "act":
            op(eng, lambda e: e.activation(out=out_ap, in_=in_ap, func=AF.Identity), reads=reads, writes=writes)
        else:
            op(eng, lambda e: e.tensor_copy(out=out_ap, in_=in_ap), reads=reads, writes=writes)

    def recip(out_ap, in_ap, reads, writes):
        op("dve", lambda e: e.reciprocal(out=out_ap, in_=in_ap), reads=reads, writes=writes)

    def rstd_from(out_slot, out_ap, in_ap, n, reads, tmp_slot, tmp_ap):
        act(tmp_ap, in_ap, AF.Sqrt, list(reads) + [epsc], [tmp_slot], bias=epsc.ap[0:out_ap.shape[0], 0:1], scale=1.0 / n)
        recip(out_ap, tmp_ap, [tmp_slot], [out_slot])

    def split_bf16(x_ap, x_slot, pf, pb, npc):
        cur_ap, cur_slot = x_ap, x_slot
        for k in range(npc):
            cp(pb.ap[:, 4 * k:4 * k + 4], cur_ap, [cur_slot, pf], [pb])
            cp(pf.ap[:, 4 * k:4 * k + 4], pb.ap[:, 4 * k:4 * k + 4], [pb], [pf])
            if k < npc - 1:
                tt(pf.ap[:, 12 + 4 * k:16 + 4 * k], cur_ap, pf.ap[:, 4 * k:4 * k + 4], ALU.subtract, [cur_slot, pf], [pf])
                cur_ap, cur_slot = pf.ap[:, 12 + 4 * k:16 + 4 * k], pf

    cst = P.alloc(C_END, "cst")
    dma("sp", cst.ap, consts_in, writes=[cst])
    identf = cst.ap[:, C_ID:C_ID + 128]
    maskI = cst.ap[:, C_MI:C_MI + 128]
    maskS = cst.ap[:, C_MS:C_MS + 128]
    onesf = cst.ap[:, C_ONE:C_ONE + 128]
    cb = P.alloc(C_END // 2, "cb", BF16)
    cp(cb.ap, cst.ap, [cst], [cb])
    identb = cb.ap[:, C_ID:C_ID + 128]
    maskIb = cb.ap[:, C_MI:C_MI + 128]
    onesb = cb.ap[:, C_ONE:C_ONE + 128]
    rotb = cb.ap[0:64, C_ROT:C_ROT + 64]
    epsc = P.alloc(8, "epsc")
    op("pool", lambda e: e.memset(epsc.ap, EPS), writes=[epsc])
    mask4I = P.alloc(512, "mask4I")
    mask4S = P.alloc(512, "mask4S")
    ident4b = P.alloc(256, "ident4b", BF16)
    mask4Ib = P.alloc(256, "mask4Ib", BF16)
    for h in range(4):
        cp(mask4Ib.ap[:, h * 128:(h + 1) * 128], maskI, [cst], [mask4Ib], eng="pool")
        cp(mask4I.ap[:, h * 128:(h + 1) * 128], maskI, [cst], [mask4I], eng="pool")
        cp(mask4S.ap[:, h * 128:(h + 1) * 128], maskS, [cst], [mask4S], eng="pool")
        cp(ident4b.ap[:, h * 128:(h + 1) * 128], identf, [cst], [ident4b], eng="pool")
    lvl4 = P.alloc(8 * 256, "lvl4", BF16)
    for m in range(8):
        for h in range(4):
            cp(lvl4.ap[:, m * 512 + h * 128:m * 512 + (h + 1) * 128], cst.ap[:, C_LVL + m * 128:C_LVL + (m + 1) * 128], [cst], [lvl4], eng="pool")
    base_mark = P.mark()

    def col(vec_ap, n=128):
        return vec_ap.rearrange("(p o) -> p o", o=1)

    stage_i = [0]

    def load_cast(dst_slot, dst_ap, src_ap, shape_words, stg):
        k = stage_i[0] % len(stg)
        stage_i[0] += 1
        s = stg[k]
        sap = s.ap[:, 0:shape_words]
        if len(dst_ap.shape) == 3:
            sap = sap.rearrange("p (a b) -> p a b", a=dst_ap.shape[1])
        dma("sp", sap, src_ap, writes=[s])
        cp(dst_ap, sap, [s], [dst_slot], eng="pool")

    for l in range(L):
        for s in range(NSEQ):
            x_src = x_in if l == 0 else out
            P.release(base_mark)
            P.pm('pa0')
            hT = P.alloc(8 * S // 2, "hT", BF16)
            hT3 = hT.ap.rearrange("p (k t) -> p k t", k=8)
            hT_t = [Slot("hT%d" % t) for t in range(NT)]
            for t_ in hT_t:
                t_.r = list(hT.r)
            pa_mark = P.mark()
            g1bc = P.alloc(D, "g1bc")
            dma("sp", g1bc.ap, W["norm1_g"][l].partition_broadcast(128), writes=[g1bc])
            xb = [P.alloc(D, "xb%d" % i) for i in range(2)]
            hb = [P.alloc(D // 2, "hb%d" % i, BF16) for i in range(2)]
            sm = [P.alloc(8, "sm%d" % i) for i in range(2)]
            junk = P.alloc(D // 2, "junk", BF16)

            def norm_transpose(x_ap, xslots, gbc, dstT3, dst_slots_fn, ntiles, col0):
                for t in range(ntiles):
                    xt, ht, st = xb[t % 2], hb[t % 2], sm[t % 2]
                    dma("sp", xt.ap, x_ap(t), reads=xslots(t), writes=[xt])
                    act(junk.ap, xt.ap, AF.Square, [xt], [junk, st], accum=st.ap[:, 0:1])
                    rstd_from(st, st.ap[:, 2:3], st.ap[:, 0:1], float(D), [st], st, st.ap[:, 1:2])
                    stt(ht.ap, xt.ap, st.ap[:, 2:3], gbc.ap, ALU.mult, ALU.mult, [xt, st, gbc], [ht])
                    pb = PS[t % 2]
                    pbv = psb(t % 2)
                    for k in range(8):
                        tr(pb, pbv[:, k * 128:(k + 1) * 128], ht.ap[:, k * 128:(k + 1) * 128], identb, [ht, cb])
                    dsl_ = dst_slots_fn(t)
                    c0 = col0 + t * 128
                    op("act", lambda e, pbv=pbv, c0=c0: e.activation(
                        out=dstT3[:, :, c0:c0 + 128], in_=pbv.rearrange("p (k t) -> p k t", k=8), func=AF.Identity),
                       reads=[pb], writes=dsl_)

            norm_transpose(lambda t: x_src[s, t * 128:(t + 1) * 128, :], lambda t: [XS[s][t]], g1bc, hT3,
                           lambda t: [hT_t[t]], NT, 0)
            P.release(pa_mark)
            if lvl < 2:
                continue
            if 'pa' not in PHASES:
                fm_skip = True
            else:
                fm_skip = False
            P.pm('pa1')
            wsf = [P.alloc(8 * 128, "wsf%d" % i) for i in range(2)]
            wsb = [P.alloc(8 * 128 // 2, "wsb%d" % i, BF16) for i in range(2)]
            ob = [P.alloc(256, "ob%d" % i, BF16) for i in range(3)]
            sqb = [P.alloc(256, "sqb%d" % i, BF16) for i in range(2)]
            rsf = [P.alloc(512, "rsf%d" % i) for i in range(2)]
            rtf = [P.alloc(512, "rtf%d" % i) for i in range(2)]
            svf = [P.alloc(512, "svf%d" % i) for i in range(2)]
            xin = [P.alloc(520, "xin%d" % i) for i in range(2)]
            accf = [P.alloc(512, "accf%d" % i) for i in range(2)]
            gcols = P.alloc(16, "gcols")
            convw = P.alloc(48, "convw")
            with nc.allow_non_contiguous_dma(reason="small param layouts"):
                for kk in range(4):
                    dma("sp", convw.ap[:, kk * 12:(kk + 1) * 12], W["dn_conv_w"][l, kk].rearrange("(g p) -> p g", p=128), writes=[convw])
                dma("sp", gcols.ap[:, 0:1], col(W["fox_q_norm_g"][l]), writes=[gcols])
                dma("sp", gcols.ap[:, 1:2], col(W["fox_k_norm_g"][l]), writes=[gcols])
                dma("sp", gcols.ap[:, 2:4], W["mla_q_norm_g"][l].rearrange("(c p) -> p c", p=128), writes=[gcols])
                dma("sp", gcols.ap[:, 4:5], col(W["mla_kv_norm_g"][l]), writes=[gcols])
            ts(gcols.ap[:, 0:1], gcols.ap[:, 0:1], 128.0 ** -0.5, ALU.mult, [gcols], [gcols])

            gi = [0]
            ei = [0]

            def fm_group(c0, ncol, kind, dst, r0, extra=None):
                if fm_skip:
                    return
                k = gi[0] % 2
                gi[0] += 1
                wf, wb = wsf[k], wsb[k]
                wf3 = wf.ap[:, 0:8 * ncol].rearrange("p (k n) -> p k n", k=8)
                wb3 = wb.ap[:, 0:8 * ncol].rearrange("p (k n) -> p k n", k=8)
                for hh in range(2):
                    dma("sp", wf3[:, hh * 4:(hh + 1) * 4, :], W["w_in"][l, hh * 512:(hh + 1) * 512, c0:c0 + ncol].rearrange("(kc kp) n -> kp kc n", kp=128), writes=[wf])
                cp(wb3, wf3, [wf], [wb], eng="pool")
                for b in range(NB):
                    e_ = ei[0]
                    ei[0] += 1
                    ps = PS[2 + e_ % 3]
                    pa = ps.ap[0:ncol, :]
                    for kc in range(8):
                        mm(ps, pa, wb3[:, kc, :], hT3[:, kc, b * 512:(b + 1) * 512], kc == 0, kc == 7,
                           [wb] + hT_t[b * 4:(b + 1) * 4])
                    o = ob[e_ % 3]
                    oa = o.ap[0:ncol, :]
                    dd = SC[dst][r0:r0 + ncol, b * 512:(b + 1) * 512]
                    dsl_ = dsl(dst, b * 4, b * 4 + 4)
                    if kind in ("silu", "gelu", "sigmoid", "copy"):
                        f = {"silu": AF.Silu, "gelu": AF.Gelu_apprx_tanh, "sigmoid": AF.Sigmoid, "copy": AF.Identity}[kind]
                        act(oa, pa, f, [ps], [o])
                    elif kind == "rms":
                        gc_ap, nfeat = extra
                        sq, rs, rt = sqb[e_ % 2], rsf[e_ % 2], rtf[e_ % 2]
                        act(sq.ap[0:ncol, :], pa, AF.Square, [ps], [sq])
                        p2 = PS[5 + e_ % 2]
                        mm(p2, p2.ap, onesb[0:ncol, :], sq.ap[0:ncol, :], True, True, [sq, cb])
                        rstd_from(rs, rs.ap, p2.ap, float(nfeat), [p2], rt, rt.ap)
                        stt(oa, pa, gc_ap, rs.ap[0:ncol, :], ALU.mult, ALU.mult, [ps, rs, gcols], [o])
                    elif kind == "conv":
                        g_idx, which = extra
                        xi, xp = xin[b % 2], xin[(b + 1) % 2]
                        act(xi.ap[:, 3:515], pa, AF.Identity, [ps], [xi])
                        if b == 0:
                            op("pool", lambda e, xi=xi: e.memset(xi.ap[:, 0:3], 0.0), writes=[xi])
                        else:
                            cp(xi.ap[:, 0:3], xp.ap[:, 512:515], [xp], [xi], eng="pool")
                        ac = accf[e_ % 2]
                        wk = [convw.ap[:, kk * 12 + g_idx:kk * 12 + g_idx + 1] for kk in range(4)]
                        ts(ac.ap, xi.ap[:, 0:512], wk[0], ALU.mult, [xi, convw], [ac])
                        for kk in range(1, 4):
                            stt(ac.ap, xi.ap[:, kk:kk + 512], wk[kk], ac.ap, ALU.mult, ALU.add, [xi, convw, ac], [ac])
                        if which == "v":
                            act(oa, ac.ap, AF.Silu, [ac], [o])
                        else:
                            sv = svf[e_ % 2]
                            act(sv.ap, ac.ap, AF.Silu, [ac], [sv])
                            sq, rs, rt = sqb[e_ % 2], rsf[e_ % 2], rtf[e_ % 2]
                            act(sq.ap, sv.ap, AF.Square, [sv], [sq])
                            p2 = PS[5 + e_ % 2]
                            mm(p2, p2.ap, onesb, sq.ap, True, True, [sq, cb])
                            rstd_from(rs, rs.ap, p2.ap, 1.0, [p2], rt, rt.ap)
                            cs = (128.0 ** -0.5) if which == "q" else 1.0
                            stt(oa, sv.ap, cs, rs.ap, ALU.mult, ALU.mult, [sv, rs], [o])
                    dma("pool", dd, oa, reads=[o], writes=dsl_)

            for g in range(12):
                which = "qkv"[g // 4]
                fm_group(g * 128, 128, "conv", {"q": "dnq", "k": "dnk", "v": "dnv"}[which], (g % 4) * 128, (g, which))
            for g in range(4):
                fm_group(1536 + g * 128, 128, "silu", "dnz", g * 128)
            for b in range(NB):
                pass
            wf, wb = wsf[gi[0] % 2], wsb[gi[0] % 2]
            gi[0] += 1
            wf2 = [wf, wsf[gi[0] % 2]]
            wb2 = [wb, wsb[gi[0] % 2]]
            gi[0] += 1
            w3 = []
            for c in range(2):
                wf3 = wf2[c].ap.rearrange("p (k n) -> p k n", k=8)
                wb3 = wb2[c].ap.rearrange("p (k n) -> p k n", k=8)
                for hh in range(2):
                    dma("sp", wf3[:, hh * 4:(hh + 1) * 4, :], W["w_in"][l, hh * 512:(hh + 1) * 512, 2056 + c * 128:2056 + (c + 1) * 128].rearrange("(kc kp) n -> kp kc n", kp=128), writes=[wf2[c]])
                cp(wb3, wf3, [wf2[c]], [wb2[c]], eng="pool")
                w3.append(wb3)
            for b in range(NB):
                pss = [PS[2], PS[3]]
                for c in range(2):
                    for kc in range(8):
                        mm(pss[c], pss[c].ap, w3[c][:, kc, :], hT3[:, kc, b * 512:(b + 1) * 512], kc == 0, kc == 7,
                           [wb2[c]] + hT_t[b * 4:(b + 1) * 4])
                p2 = PS[5]
                for c in range(2):
                    act(sqb[c].ap, pss[c].ap, AF.Square, [pss[c]], [sqb[c]])
                    mm(p2, p2.ap, onesb, sqb[c].ap, c == 0, c == 1, [sqb[c], cb])
                rstd_from(rsf[0], rsf[0].ap, p2.ap, 256.0, [p2], rtf[0], rtf[0].ap)
                for c in range(2):
                    stt(ob[c].ap, pss[c].ap, gcols.ap[:, 2 + c:3 + c], rsf[0].ap, ALU.mult, ALU.mult, [pss[c], rsf[0], gcols], [ob[c]])
                    dma("pool", SC["cq"][c * 128:(c + 1) * 128, b * 512:(b + 1) * 512], ob[c].ap, reads=[ob[c]],
                        writes=dsl("cq", b * 4, b * 4 + 4))
            fm_group(2312, 128, "rms", "ckv", 0, (gcols.ap[:, 4:5], 128))
            fm_group(2440, 64, "copy", "kr", 0)
            for g in range(4):
                fm_group(2504 + g * 128, 128, "gelu", "sgu", g * 128)
            for g in range(4):
                fm_group(3528 + g * 128, 128, "rms", "fq", g * 128, (gcols.ap[:, 0:1], 128))
            for g in range(4):
                fm_group(3528 + 512 + g * 128, 128, "rms", "fk", g * 128, (gcols.ap[:, 1:2], 128))
            for g in range(32):
                fm_group(5068 + g * 128, 128, "sigmoid", "gates", g * 128)
            if lvl < 3:
                continue
            P.pm('pa2')
            m2 = P.mark()
            wtf = P.alloc(8 * 512, "wtf")
            wtb = P.alloc(8 * 512 // 2, "wtb", BF16)
            tob = [P.alloc(256, "tob%d" % i, BF16) for i in range(2)]
            tof = [P.alloc(512, "tof%d" % i) for i in range(2)]
            sgg = P.alloc(512, "sgg")
            dma("sp", sgg.ap, W["sg_v_norm_g"][l].rearrange("g c -> (g c)").partition_broadcast(128), writes=[sgg])
            ss4 = [P.alloc(16, "ss4%d" % i) for i in range(2)]

            def tm_group(cols, kind, dst):
                if fm_skip:
                    return
                ncol = sum(n for _, n in cols)
                wf3 = wtf.ap[:, 0:8 * ncol].rearrange("p (k n) -> p k n", k=8)
                wb3 = wtb.ap[:, 0:8 * ncol].rearrange("p (k n) -> p k n", k=8)
                o0 = 0
                with nc.allow_non_contiguous_dma(reason="narrow weight columns"):
                    for c0, n in cols:
                        for hh in range(2):
                            dma("sp", wf3[:, hh * 4:(hh + 1) * 4, o0:o0 + n], W["w_in"][l, hh * 512:(hh + 1) * 512, c0:c0 + n].rearrange("(kc kp) n -> kp kc n", kp=128), writes=[wtf])
                        o0 += n
                cp(wb3, wf3, [wtf], [wtb], eng="pool")
                for t in range(NT):
                    ps = PS[2 + t % 3]
                    pa = ps.ap[:, 0:ncol]
                    for kc in range(8):
                        mm(ps, pa, hT3[:, kc, t * 128:(t + 1) * 128], wb3[:, kc, :], kc == 0, kc == 7, [wtb, hT_t[t]])
                    if kind == "copyb":
                        o = tob[t % 2]
                        act(o.ap, pa, AF.Identity, [ps], [o])
                        dma("pool", SC[dst][t * 128:(t + 1) * 128, :], o.ap, reads=[o], writes=dsl(dst, t))
                    elif kind == "copyf":
                        o = tof[t % 2]
                        act(o.ap[:, 0:ncol], pa, AF.Identity, [ps], [o])
                        dma("pool", SC[dst][t * 128:(t + 1) * 128, :], o.ap[:, 0:ncol], reads=[o], writes=dsl(dst, t))
                    elif kind == "sgv":
                        gv, s4, o = tof[t % 2], ss4[t % 2], tob[t % 2]
                        act(gv.ap, pa, AF.Gelu_apprx_tanh, [ps], [gv])
                        for g in range(4):
                            act(junk.ap[:, 0:128], gv.ap[:, g * 128:(g + 1) * 128], AF.Square, [gv], [junk, s4], accum=s4.ap[:, g:g + 1])
                        rstd_from(s4, s4.ap[:, 8:12], s4.ap[:, 0:4], 128.0, [s4], s4, s4.ap[:, 4:8])
                        for g in range(4):
                            stt(o.ap[:, g * 128:(g + 1) * 128], gv.ap[:, g * 128:(g + 1) * 128], s4.ap[:, 8 + g:9 + g],
                                sgg.ap[:, g * 128:(g + 1) * 128], ALU.mult, ALU.mult, [gv, s4, sgg], [o])
                        dma("pool", SC[dst][t * 128:(t + 1) * 128, :], o.ap, reads=[o], writes=dsl(dst, t))

            tm_group([(3016, 512)], "sgv", "sgv")
            tm_group([(3528 + 1024, 512)], "copyb", "fv")
            tm_group([(2048, 8), (5064, 4)], "copyf", "gsm")
            P.release(base_mark)

            if lvl < 4:
                continue
            P.pm('gdn')
            if 'gdn' in PHASES:
                phase_gdn(nc, P, l, s, locals())
            else:
                fold(hT, hT_t)
            P.release(base_mark)
            if lvl < 5:
                continue
            P.pm('mla')
            if 'mla' in PHASES:
                phase_mla(nc, P, l, s, locals())
            P.release(base_mark)
            if lvl < 6:
                continue
            P.pm('sg')
            if 'sg' in PHASES:
                phase_sg(nc, P, l, s, locals())
            P.release(base_mark)
            if lvl < 7:
                continue
            P.pm('fox')
            if 'fox' in PHASES:
                phase_fox(nc, P, l, s, locals())
            P.release(base_mark)
            if lvl < 8:
                continue
            P.pm('merge')
            if 'mf' in PHASES:
                phase_merge_ffn(nc, P, l, s, locals())
    P.pm('end')
    P.emit()
    nc._phase_marks = P.phase_marks
    return nc


def fold(parent, children):
    for c in children:
        if c.w is not None:
            parent.r.append(c.w)
        parent.r.extend(c.r)


class Env:
    def __init__(self, d):
        self.__dict__.update(d)


def phase_gdn(nc, P, l, s, env):
    E = Env(env)
    fold(E.hT, E.hT_t)
    W, SC, PS, NT = E.W, E.SC, E.PS, E.NT
    op, dma, mm, tr, act, tt, ts, stt, cp, recip, dsl, psb = E.op, E.dma, E.mm, E.tr, E.act, E.tt, E.ts, E.stt, E.cp, E.recip, E.dsl, E.psb
    cst, cb = E.cst, E.cb
    A = P.alloc
    prm = A(32, "gprm")
    dma("sp", prm.ap[:, 0:4], W["dn_dt_bias"][l].partition_broadcast(128), writes=[prm])
    dma("sp", prm.ap[:, 4:8], W["dn_a_log"][l].partition_broadcast(128), writes=[prm])
    with nc.allow_non_contiguous_dma(reason="col"):
        dma("sp", prm.ap[:, 8:9], E.col(W["dn_out_norm_g"][l]), writes=[prm])
    act(prm.ap[:, 4:8], prm.ap[:, 4:8], AF.Exp, [prm], [prm])
    ts(prm.ap[:, 4:8], prm.ap[:, 4:8], -1.0, ALU.mult, [prm], [prm])
    Sf = A(512, "Sf")
    Sb = A(256, "Sb", BF16)
    op("pool", lambda e: e.memset(Sf.ap, 0.0), writes=[Sf])
    op("pool", lambda e: e.memset(Sb.ap, 0.0), writes=[Sb])

    def buf2(words, name, dt=None):
        return [A(words, "%s%d" % (name, i), dt) for i in range(2)]
    qT, kT, vT, zs = buf2(256, "gq", BF16), buf2(256, "gk", BF16), buf2(256, "gv", BF16), buf2(256, "gz", BF16)
    gs = buf2(32, "gs")
    rgb = buf2(512, "rgb", BF16)
    gpf = buf2(24, "gpf")
    gpb = buf2(8, "gpb", BF16)
    ET = buf2(512, "ET")
    DT = buf2(512, "DT")
    DTs = buf2(512, "DTs")
    EG = buf2(512, "EG")
    aqk = buf2(256, "aqk", BF16)
    Xb = buf2(256, "Xb", BF16)
    Pa = buf2(256, "Pa", BF16)
    PaT = buf2(256, "PaT", BF16)
    TT = buf2(256, "TT", BF16)
    Tb = buf2(256, "Tb", BF16)
    Mb = A(256, "Mb", BF16)
    Mb2 = A(256, "Mb2", BF16)
    Cm = [A(256, "Cm%d" % i, BF16) for i in range(6)]
    rhu = buf2(256, "rhu", BF16)
    rhw = buf2(256, "rhw", BF16)
    kde = buf2(256, "kde", BF16)
    uf = buf2(512, "uf")
    wT = buf2(256, "wT", BF16)
    qd = buf2(256, "qd", BF16)
    vn = buf2(256, "vn", BF16)
    sqo = buf2(256, "sqo", BF16)
    rso = buf2(512, "rso")
    rto = buf2(512, "rto")
    of = buf2(512, "of")
    ob_ = buf2(256, "gob", BF16)

    def h4(ap):
        return ap.rearrange("p (h t) -> p h t", h=4)

    for t in range(NT):
        i2 = t % 2
        sl = slice(t * 128, (t + 1) * 128)
        for buf, nm in ((qT, "dnq"), (kT, "dnk"), (vT, "dnv"), (zs, "dnz")):
            dma("sp", h4(buf[i2].ap), SC[nm][:, sl].rearrange("(h p) t -> p h t", p=128), reads=dsl(nm, t), writes=[buf[i2]])
        g_ = gs[i2]
        dma("sp", g_.ap[:, 0:12], SC["gsm"][sl, :], reads=dsl("gsm", t), writes=[g_])
        tt(g_.ap[:, 12:16], g_.ap[:, 0:4], prm.ap[:, 0:4], ALU.add, [g_, prm], [g_])
        act(g_.ap[:, 12:16], g_.ap[:, 12:16], AF.Exp, [g_], [g_])
        act(g_.ap[:, 12:16], g_.ap[:, 12:16], AF.Ln, [g_], [g_], bias=1.0)
        tt(g_.ap[:, 12:16], g_.ap[:, 12:16], prm.ap[:, 4:8], ALU.mult, [g_, prm], [g_])
        act(g_.ap[:, 16:20], g_.ap[:, 4:8], AF.Sigmoid, [g_], [g_])
        pf_, pb_ = gpf[i2], gpb[i2]
        E.split_bf16(g_.ap[:, 12:16], g_, pf_, pb_, 3)
        pg = PS[0]
        for k in range(3):
            mm(pg, pg.ap[:, 0:4], E.maskIb, pb_.ap[:, 4 * k:4 * k + 4], k == 0, k == 2, [cb, pb_])
        ts(g_.ap[:, 20:24], pg.ap[:, 0:4], -1.0, ALU.mult, [pg], [g_])
        pr = PS[1]
        for h in range(4):
            for k in range(2):
                rgk = rgb[i2].ap[:, (h * 2 + k) * 128:(h * 2 + k + 1) * 128]
                ts(rgk, E.maskIb, pf_.ap[:, 4 * k + h:4 * k + h + 1], ALU.mult, [cb, pf_], [rgb[i2]], eng=("dve" if k == 0 else "pool"))
                mm(pr, pr.ap[:, h * 128:(h + 1) * 128], E.onesb, rgk, k == 0, k == 1, [cb, rgb[i2]])
        cp(g_.ap[:, 24:28], h4(pr.ap)[:, :, 127], [pr], [g_])
        tt(g_.ap[:, 28:32], g_.ap[:, 24:28], g_.ap[:, 20:24], ALU.add, [g_], [g_])
        for h in range(4):
            ts(ET[i2].ap[:, h * 128:(h + 1) * 128], pr.ap[:, h * 128:(h + 1) * 128], g_.ap[:, 20 + h:21 + h], ALU.add, [pr, g_], [ET[i2]],
               s2=0.0, alu2=ALU.min)
        act(ET[i2].ap, ET[i2].ap, AF.Exp, [ET[i2]], [ET[i2]])
        act(EG[i2].ap, pr.ap, AF.Exp, [pr], [EG[i2]])
        tt(DT[i2].ap, ET[i2].ap, E.mask4I.ap, ALU.mult, [ET[i2], E.mask4I], [DT[i2]])
        tt(DTs[i2].ap, DT[i2].ap, E.mask4S.ap, ALU.mult, [DT[i2], E.mask4S], [DTs[i2]], eng="pool")
        g2 = gs[i2]
        act(g_.ap[:, 28:32], g_.ap[:, 28:32], AF.Exp, [g_], [g_])
        act(g_.ap[:, 24:28], g_.ap[:, 24:28], AF.Exp, [g_], [g_])
        act(g_.ap[:, 0:4], g_.ap[:, 20:24], AF.Exp, [g_], [g_], scale=-1.0)
        tt(g_.ap[:, 0:4], g_.ap[:, 0:4], g_.ap[:, 16:20], ALU.mult, [g_], [g_])
        ts(g_.ap[:, 4:8], g_.ap[:, 16:20], -1.0, ALU.mult, [g_], [g_])
        if GSTOP <= 1:
            continue
        pq, pk = PS[2], PS[3]
        q3, k3, v3 = h4(qT[i2].ap), h4(kT[i2].ap), h4(vT[i2].ap)
        for h in range(4):
            mm(pq, pq.ap[:, h * 128:(h + 1) * 128], k3[:, h, :], q3[:, h, :], True, True, [kT[i2], qT[i2]])
            mm(pk, pk.ap[:, h * 128:(h + 1) * 128], k3[:, h, :], k3[:, h, :], True, True, [kT[i2]])
        tt(aqk[i2].ap, pq.ap, DT[i2].ap, ALU.mult, [pq, DT[i2]], [aqk[i2]])
        tt(Xb[i2].ap, pk.ap, DTs[i2].ap, ALU.mult, [pk, DTs[i2]], [Xb[i2]])
        if GSTOP <= 2:
            continue
        pt = PS[4]
        ptb = psb(4)
        for h in range(4):
            tr(pt, ptb[:, h * 128:(h + 1) * 128], Xb[i2].ap[:, h * 128:(h + 1) * 128], E.identb, [Xb[i2], cb])
        cur, curT, curTT = Pa[0], PaT[0], TT[0]
        for h in range(4):
            ts(cur.ap[:, h * 128:(h + 1) * 128], ptb[:, h * 128:(h + 1) * 128], g_.ap[:, 4 + h:5 + h], ALU.mult, [pt, g_], [cur])
        pt2 = PS[5]
        pt2b = psb(5)
        if GVAR == 2:
            continue
        for h in range(4):
            tr(pt2, pt2b[:, h * 128:(h + 1) * 128], cur.ap[:, h * 128:(h + 1) * 128], E.identb, [cur, cb])
        if GVAR == 1:
            continue
        cp(curT.ap, pt2b[:, 0:512], [pt2], [curT], eng="act")
        if GVAR == 3:
            continue
        if GSTOP <= 3:
            continue
        pv, pkk = PS[6], PS[7]
        pvb, pkb = psb(6), psb(7)
        for h in range(4):
            tr(pv, pvb[:, h * 128:(h + 1) * 128], v3[:, h, :], E.identb, [vT[i2], cb])
            tr(pkk, pkb[:, h * 128:(h + 1) * 128], k3[:, h, :], E.identb, [kT[i2], cb])
        for h in range(4):
            hs = slice(h * 128, (h + 1) * 128)
            ts(rhu[i2].ap[:, hs], pvb[:, hs], g_.ap[:, 16 + h:17 + h], ALU.mult, [pv, g_], [rhu[i2]])
            ts(rhw[i2].ap[:, hs], pkb[:, hs], g_.ap[:, 0 + h:1 + h], ALU.mult, [pkk, g_], [rhw[i2]])
            ts(kde[i2].ap[:, hs], pkb[:, hs], g_.ap[:, 28 + h:29 + h], ALU.mult, [pkk, g_], [kde[i2]])
        if GSTOP <= 4:
            continue
        def lv(m):
            return E.lvl4.ap[:, m * 512:(m + 1) * 512]
        Tc, TTc = Tb[0], TT[1]
        tt(Mb.ap, cur.ap, lv(7), ALU.mult, [cur, E.lvl4], [Mb], eng="pool")
        tt(Tc.ap, Mb.ap, E.ident4b.ap, ALU.add, [Mb, E.ident4b], [Tc], eng="pool")
        tt(Mb2.ap, curT.ap, lv(0), ALU.mult, [curT, E.lvl4], [Mb2])
        tt(TTc.ap, Mb2.ap, E.ident4b.ap, ALU.add, [Mb2, E.ident4b], [TTc])
        for m in range(1, 7):
            tt(Cm[m - 1].ap, curT.ap, lv(m), ALU.mult, [curT, E.lvl4], [Cm[m - 1]], eng="pool")
        for m in range(1, 7):
            pM, pU, pV = PS[4], PS[5], PS[6]
            for h in range(4):
                hs = slice(h * 128, (h + 1) * 128)
                mm(pM, pM.ap[:, hs], Cm[m - 1].ap[:, hs], Tc.ap[:, hs], True, True, [Cm[m - 1], Tc])
            cp(Mb.ap, pM.ap, [pM], [Mb], eng="act")
            Tn, TTn = Tb[m % 2], TT[m % 2]
            for h in range(4):
                hs = slice(h * 128, (h + 1) * 128)
                if m < 6:
                    mm(pU, pU.ap[:, hs], E.identb, Tc.ap[:, hs], True, False, [cb, Tc])
                    mm(pU, pU.ap[:, hs], TTc.ap[:, hs], Mb.ap[:, hs], False, True, [TTc, Mb])
                mm(pV, pV.ap[:, hs], E.identb, TTc.ap[:, hs], True, False, [cb, TTc])
                mm(pV, pV.ap[:, hs], Mb.ap[:, hs], TTc.ap[:, hs], False, True, [TTc, Mb])
            if m < 6:
                cp(Tn.ap, pU.ap, [pU], [Tn], eng="act")
            cp(TTn.ap, pV.ap, [pV], [TTn])
            Tc, TTc = Tn, TTn
        curTT = TTc
        pu, pw = PS[4], PS[5]
        for h in range(4):
            hs = slice(h * 128, (h + 1) * 128)
            mm(pu, pu.ap[:, hs], curTT.ap[:, hs], rhu[i2].ap[:, hs], True, True, [curTT, rhu[i2]])
            mm(pw, pw.ap[:, hs], rhw[i2].ap[:, hs], curTT.ap[:, hs], True, True, [curTT, rhw[i2]])
        cp(uf[i2].ap, pu.ap, [pu], [uf[i2]], eng="act")
        cp(wT[i2].ap, pw.ap, [pw], [wT[i2]])
        tt(qd[i2].ap, qT[i2].ap, EG[i2].ap, ALU.mult, [qT[i2], EG[i2]], [qd[i2]], eng="pool")
        if GSTOP <= 6:
            continue
        p1, p2, p3 = PS[6], PS[7], PS[0]
        for h in range(4):
            hs = slice(h * 128, (h + 1) * 128)
            mm(p1, p1.ap[:, hs], wT[i2].ap[:, hs], Sb.ap[:, hs], True, True, [wT[i2], Sb])
        tt(vn[i2].ap, uf[i2].ap, p1.ap, ALU.subtract, [uf[i2], p1], [vn[i2]])
        for h in range(4):
            hs = slice(h * 128, (h + 1) * 128)
            mm(p2, p2.ap[:, hs], Sb.ap[:, hs], qd[i2].ap[:, hs], True, False, [Sb, qd[i2]])
            mm(p2, p2.ap[:, hs], vn[i2].ap[:, hs], aqk[i2].ap[:, hs], False, True, [vn[i2], aqk[i2]])
            mm(p3, p3.ap[:, hs], kde[i2].ap[:, hs], vn[i2].ap[:, hs], True, True, [kde[i2], vn[i2]])
        for h in range(4):
            hs = slice(h * 128, (h + 1) * 128)
            stt(Sf.ap[:, hs], Sf.ap[:, hs], g_.ap[:, 24 + h:25 + h], p3.ap[:, hs], ALU.mult, ALU.add, [Sf, g_, p3], [Sf])
        cp(Sb.ap, Sf.ap, [Sf], [Sb], eng="act")
        if GSTOP <= 7:
            continue
        act(sqo[i2].ap, p2.ap, AF.Square, [p2], [sqo[i2]])
        po = PS[1]
        mm(po, po.ap, E.onesb, sqo[i2].ap, True, True, [cb, sqo[i2]])
        E.rstd_from(rso[i2], rso[i2].ap, po.ap, 128.0, [po], rto[i2], rto[i2].ap)
        tt(of[i2].ap, p2.ap, rso[i2].ap, ALU.mult, [p2, rso[i2]], [of[i2]])
        stt(ob_[i2].ap, of[i2].ap, prm.ap[:, 8:9], zs[i2].ap, ALU.mult, ALU.mult, [of[i2], prm, zs[i2]], [ob_[i2]])
        dma("pool", SC["br"][0:512, sl].rearrange("(h p) t -> p h t", p=128), h4(ob_[i2].ap), reads=[ob_[i2]], writes=dsl("br", t))


def attention(nc, P, E, qsrc, ksrc, vname, h, kchunks, bias_fn, br_row0):
    SC, PS, NT, NB, S = E.SC, E.PS, E.NT, E.NB, E.S
    op, dma, mm, act, tt, cp, recip, dsl = E.op, E.dma, E.mm, E.act, E.tt, E.cp, E.recip, E.dsl
    A = P.alloc
    m0 = P.mark()
    qs, ks = [], []
    for ci, (r0, nr) in enumerate(kchunks):
        q_ = A(S // 2, "aq%d" % ci, BF16)
        k_ = A(S // 2, "ak%d" % ci, BF16)
        dma("sp", q_.ap[0:nr, :], qsrc[r0:r0 + nr, :], reads=E.q_slots, writes=[q_])
        dma("sp", k_.ap[0:nr, :], ksrc[r0:r0 + nr, :], reads=E.k_slots, writes=[k_])
        qs.append((q_, nr))
        ks.append((k_, nr))
    v_ = A(S // 2, "av", BF16)
    v3 = v_.ap.rearrange("p (t d) -> p t d", t=NT)
    for t4 in range(0, NT, 4):
        dma("sp", v3[:, t4:t4 + 4, :], SC[vname][t4 * 128:(t4 + 4) * 128, h * 128:(h + 1) * 128].rearrange("(t p) d -> p t d", p=128),
            reads=dsl(vname, t4, t4 + 4), writes=[v_])
    pT = [A(256, "pT%d" % i, BF16) for i in range(3)]
    rl = [A(512, "rl%d" % i) for i in range(2)]
    oo = [A(256, "oo%d" % i, BF16) for i in range(2)]
    bt = [A(NT, "bt%d" % i) for i in range(2)] if bias_fn else None
    pairs = [(J, i) for J in range(NB) for i in range(4 * J + 4)]
    st = {}

    def stage_a(idx):
        J, i = pairs[idx]
        lo = max(0, i - 4 * J)
        n0 = lo * 128
        ps = PS[4 + idx % 3]
        p_ = pT[idx % 3]
        for ci in range(len(qs)):
            (q_, nr), (k_, _) = qs[ci], ks[ci]
            mm(ps, ps.ap[:, n0:512], k_.ap[0:nr, i * 128:(i + 1) * 128], q_.ap[0:nr, J * 512 + n0:(J + 1) * 512],
               ci == 0, ci == len(qs) - 1, [q_, k_])
        if bias_fn is None:
            act(p_.ap[:, n0:512], ps.ap[:, n0:512], AF.Exp, [ps], [p_])
        else:
            b_ = bt[idx % 2]
            bias_fn(b_, i)
            for jj in range(lo, 4):
                act(p_.ap[:, jj * 128:(jj + 1) * 128], ps.ap[:, jj * 128:(jj + 1) * 128], AF.Exp, [ps, b_], [p_],
                    bias=b_.ap[:, 4 * J + jj:4 * J + jj + 1])
        if i >= 4 * J:
            tt(p_.ap[:, n0:n0 + 128], p_.ap[:, n0:n0 + 128], E.maskIb, ALU.mult, [p_, E.cb], [p_], eng="pool")

    def stage_b(idx):
        J, i = pairs[idx]
        lo = max(0, i - 4 * J)
        n0 = lo * 128
        last = 4 * J + 3
        p_ = pT[idx % 3]
        po, pl = PS[(J % 2) * 2], PS[(J % 2) * 2 + 1]
        mm(po, po.ap[:, n0:512], v3[:, i, :], p_.ap[:, n0:512], i == 0, i == last, [v_, p_])
        mm(pl, pl.ap[:, n0:512], E.onesb, p_.ap[:, n0:512], i == 0, i == last, [E.cb, p_])
        if i == last:
            r_, o_ = rl[J % 2], oo[J % 2]
            recip(r_.ap, pl.ap, [pl], [r_])
            tt(o_.ap, po.ap, r_.ap, ALU.mult, [po, r_], [o_])
            dma("pool", SC["br"][br_row0:br_row0 + 128, J * 512:(J + 1) * 512], o_.ap, reads=[o_], writes=dsl("br", J * 4, J * 4 + 4))

    LAG = int(os.environ.get('ALAG', '2'))
    for idx in range(len(pairs) + LAG):
        if idx < len(pairs):
            stage_a(idx)
        if idx >= LAG:
            stage_b(idx - LAG)
    P.release(m0)


def phase_mla(nc, P, l, s, env):
    E = Env(env)
    W, SC, PS, NT, NB, S = E.W, E.SC, E.PS, E.NT, E.NB, E.S
    op, dma, mm, tr, act, tt, ts, stt, cp, recip, dsl, psb = E.op, E.dma, E.mm, E.tr, E.act, E.tt, E.ts, E.stt, E.cp, E.recip, E.dsl, E.psb
    cst, cb = E.cst, E.cb
    A = P.alloc
    wuq = A(2 * 768 // 2, "wuq", BF16)
    wuq3 = wuq.ap.rearrange("p (k n) -> p k n", k=2)
    wukv = A(1024 // 2, "wukv", BF16)
    stg = [A(2 * 768, "mstg%d" % i) for i in range(2)]
    E.load_cast(wuq, wuq3, W["mla_w_uq"][l].rearrange("(k p) n -> p k n", p=128), 2 * 768, stg)
    E.load_cast(wukv, wukv.ap, W["mla_w_ukv"][l], 1024, stg)
    wv = A(256, "wv", BF16)
    for h in range(4):
        cp(wv.ap[:, h * 128:(h + 1) * 128], wukv.ap[:, h * 256 + 128:h * 256 + 256], [wukv], [wv], eng="pool")
    gq = A(8, "gqk")
    with nc.allow_non_contiguous_dma(reason="cols"):
        dma("sp", gq.ap[:, 0:1], E.col(W["mla_qk_q_g"][l, 0:128]), writes=[gq])
        dma("sp", gq.ap[0:64, 1:2], E.col(W["mla_qk_q_g"][l, 128:192]), writes=[gq])
        dma("sp", gq.ap[:, 2:3], E.col(W["mla_qk_k_g"][l, 0:128]), writes=[gq])
        dma("sp", gq.ap[0:64, 3:4], E.col(W["mla_qk_k_g"][l, 128:192]), writes=[gq])
    sc = 192.0 ** -0.5
    ts(gq.ap[:, 0:1], gq.ap[:, 0:1], sc, ALU.mult, [gq], [gq])
    ts(gq.ap[0:64, 1:2], gq.ap[0:64, 1:2], sc, ALU.mult, [gq], [gq])

    def b2(words, name, dt=None, n=2):
        return [A(words, "%s%d" % (name, i), dt) for i in range(n)]
    cqb, ckvb, krb = b2(512, "cqb", BF16), b2(256, "ckvb", BF16), b2(256, "krb", BF16)
    posi = A(512, "posi", I32)
    ang = A(512, "ang")
    cos2, sin2 = A(512, "cos2"), A(512, "sin2")
    sq1, sq2, sq3 = b2(256, "msq1", BF16), b2(256, "msq2", BF16), b2(256, "msq3", BF16)
    rs, rt = b2(512, "mrs"), b2(512, "mrt")
    ykr, rk, t1, t2 = A(512, "ykr"), A(512, "rk"), b2(512, "mt1"), b2(512, "mt2")
    ykrb = A(256, "ykrb", BF16)
    yq, yqb = b2(512, "yq"), b2(256, "yqb", BF16)
    o1, o2, o3 = b2(256, "mo1", BF16, 3), b2(256, "mo2", BF16, 3), b2(256, "mo3", BF16, 3)
    PI = math.pi
    mpi = A(8, "mpi")
    op("pool", lambda e: e.memset(mpi.ap, -PI), writes=[mpi])
    cnt = 0
    for b in range(NB):
        bs = slice(b * 512, (b + 1) * 512)
        cq_, ckv_, kr_ = cqb[b % 2], ckvb[b % 2], krb[b % 2]
        cq3 = cq_.ap.rearrange("p (k t) -> p k t", k=2)
        dma("sp", cq3, SC["cq"][:, bs].rearrange("(k p) t -> p k t", p=128), reads=dsl("cq", b * 4, b * 4 + 4), writes=[cq_])
        dma("sp", ckv_.ap, SC["ckv"][:, bs], reads=dsl("ckv", b * 4, b * 4 + 4), writes=[ckv_])
        dma("sp", kr_.ap[0:64, :], SC["kr"][:, bs], reads=dsl("kr", b * 4, b * 4 + 4), writes=[kr_])
        dma("sp", posi.ap[0:64, :], E.pos_in[s, bs].partition_broadcast(64), writes=[posi])
        cp(ang.ap[0:64, :], posi.ap[0:64, :], [posi], [ang])
        ts(ang.ap[0:64, :], ang.ap[0:64, :], cst.ap[0:64, C_INVF:C_INVF + 1], ALU.mult, [ang, cst], [ang])
        for tab, phi in ((cos2, 0.5 * PI), (sin2, 0.0)):
            ta_ = tab.ap[0:64, :]
            ts(ta_, ang.ap[0:64, :], 1.0 / (2 * PI), ALU.mult, [ang], [tab], s2=phi / (2 * PI) + 0.5, alu2=ALU.add)
            cp(posi.ap[0:64, :], ta_, [tab], [posi])
            cp(t1[0].ap[0:64, :], posi.ap[0:64, :], [posi], [t1[0]])
            tt(ta_, ta_, t1[0].ap[0:64, :], ALU.subtract, [tab, t1[0]], [tab])
            ts(t1[0].ap[0:64, :], ta_, 0.0, ALU.is_lt, [tab], [t1[0]])
            tt(ta_, ta_, t1[0].ap[0:64, :], ALU.add, [tab, t1[0]], [tab])
            act(ta_, ta_, AF.Sin, [tab, mpi], [tab], bias=mpi.ap[0:64, 0:1], scale=2 * PI)
        ksq = sq3[b % 2]
        act(ksq.ap[0:64, :], kr_.ap[0:64, :], AF.Square, [kr_], [ksq])
        ts(ykr.ap[0:64, :], kr_.ap[0:64, :], gq.ap[0:64, 3:4], ALU.mult, [kr_, gq], [ykr])
        cp(ykrb.ap[0:64, :], ykr.ap[0:64, :], [ykr], [ykrb], eng="pool")
        pr = PS[0]
        mm(pr, pr.ap[0:64, :], E.rotb, ykrb.ap[0:64, :], True, True, [cb, ykrb])
        tt(rk.ap[0:64, :], ykr.ap[0:64, :], cos2.ap[0:64, :], ALU.mult, [ykr, cos2], [rk])
        tt(t1[0].ap[0:64, :], pr.ap[0:64, :], sin2.ap[0:64, :], ALU.mult, [pr, sin2], [t1[0]])
        tt(rk.ap[0:64, :], rk.ap[0:64, :], t1[0].ap[0:64, :], ALU.add, [rk, t1[0]], [rk])
        for tq in range(4):
            t = b * 4 + tq
            pv = PS[1]
            mm(pv, pv.ap, ckv_.ap[:, tq * 128:(tq + 1) * 128], wv.ap, True, True, [ckv_, wv])
            o_ = o3[t % 3]
            act(o_.ap, pv.ap, AF.Identity, [pv], [o_])
            dma("pool", SC["mv"][t * 128:(t + 1) * 128, :], o_.ap, reads=[o_], writes=dsl("mv", t))
        for h in range(4):
            c2 = cnt % 2
            cnt += 1
            pqn, pqr, pkn, pss = PS[2], PS[3], PS[4], PS[5 + c2]
            for kc in range(2):
                mm(pqn, pqn.ap, wuq3[:, kc, h * 192:h * 192 + 128], cq3[:, kc, :], kc == 0, kc == 1, [wuq, cq_])
            for kc in range(2):
                mm(pqr, pqr.ap[0:64, :], wuq3[:, kc, h * 192 + 128:h * 192 + 192], cq3[:, kc, :], kc == 0, kc == 1, [wuq, cq_])
            mm(pkn, pkn.ap, wukv.ap[:, h * 256:h * 256 + 128], ckv_.ap, True, True, [wukv, ckv_])
            s1, s2 = sq1[c2], sq2[c2]
            act(s1.ap, pqn.ap, AF.Square, [pqn], [s1])
            act(s2.ap[0:64, :], pqr.ap[0:64, :], AF.Square, [pqr], [s2])
            mm(pss, pss.ap, E.onesb, s1.ap, True, False, [cb, s1])
            mm(pss, pss.ap, E.onesb[0:64, :], s2.ap[0:64, :], False, True, [cb, s2])
            E.rstd_from(rs[c2], rs[c2].ap, pss.ap, 192.0, [pss], rt[c2], rt[c2].ap)
            oq = o1[cnt % 3]
            stt(oq.ap, pqn.ap, gq.ap[:, 0:1], rs[c2].ap, ALU.mult, ALU.mult, [pqn, gq, rs[c2]], [oq])
            dma("pool", SC["mq"][h * 192:h * 192 + 128, bs], oq.ap, reads=[oq], writes=dsl("mq", b * 4, b * 4 + 4))
            y_, yb_ = yq[c2], yqb[c2]
            stt(y_.ap[0:64, :], pqr.ap[0:64, :], gq.ap[0:64, 1:2], rs[c2].ap[0:64, :], ALU.mult, ALU.mult, [pqr, gq, rs[c2]], [y_])
            cp(yb_.ap[0:64, :], y_.ap[0:64, :], [y_], [yb_], eng="pool")
            pr2 = PS[7]
            mm(pr2, pr2.ap[0:64, :], E.rotb, yb_.ap[0:64, :], True, True, [cb, yb_])
            ta, tb = t1[1], t2[c2]
            tt(ta.ap[0:64, :], y_.ap[0:64, :], cos2.ap[0:64, :], ALU.mult, [y_, cos2], [ta])
            tt(tb.ap[0:64, :], pr2.ap[0:64, :], sin2.ap[0:64, :], ALU.mult, [pr2, sin2], [tb])
            oq2 = o2[cnt % 3]
            tt(oq2.ap[0:64, :], ta.ap[0:64, :], tb.ap[0:64, :], ALU.add, [ta, tb], [oq2])
            dma("pool", SC["mq"][h * 192 + 128:h * 192 + 192, bs], oq2.ap[0:64, :], reads=[oq2], writes=dsl("mq", b * 4, b * 4 + 4))
            s1k = sq1[c2]
            act(s1k.ap, pkn.ap, AF.Square, [pkn], [s1k])
            pss2 = PS[1]
            mm(pss2, pss2.ap, E.onesb, s1k.ap, True, False, [cb, s1k])
            mm(pss2, pss2.ap, E.onesb[0:64, :], ksq.ap[0:64, :], False, True, [cb, ksq])
            E.rstd_from(rs[c2], rs[c2].ap, pss2.ap, 192.0, [pss2], rt[c2], rt[c2].ap)
            ok = o1[(cnt + 1) % 3]
            stt(ok.ap, pkn.ap, gq.ap[:, 2:3], rs[c2].ap, ALU.mult, ALU.mult, [pkn, gq, rs[c2]], [ok])
            dma("pool", SC["mk"][h * 192:h * 192 + 128, bs], ok.ap, reads=[ok], writes=dsl("mk", b * 4, b * 4 + 4))
            ok2 = o2[(cnt + 1) % 3]
            tt(ok2.ap[0:64, :], rk.ap[0:64, :], rs[c2].ap[0:64, :], ALU.mult, [rk, rs[c2]], [ok2])
            dma("pool", SC["mk"][h * 192 + 128:h * 192 + 192, bs], ok2.ap[0:64, :], reads=[ok2], writes=dsl("mk", b * 4, b * 4 + 4))
    P.release(E.base_mark)
    E.q_slots = dsl("mq", 0, NT)
    E.k_slots = dsl("mk", 0, NT)
    for h in range(4):
        attention(nc, P, E, SC["mq"][h * 192:(h + 1) * 192, :], SC["mk"][h * 192:(h + 1) * 192, :], "mv", h,
                  [(0, 128), (128, 64)], None, 512 + h * 128)


def phase_sg(nc, P, l, s, env):
    E = Env(env)
    W, SC, PS, NT = E.W, E.SC, E.PS, E.NT
    op, dma, mm, tr, act, tt, ts, stt, cp, dsl = E.op, E.dma, E.mm, E.tr, E.act, E.tt, E.ts, E.stt, E.cp, E.dsl
    A = P.alloc
    wsf_ = A(512, "sgwf")
    wsT = A(256, "sgwT", BF16)
    bbc = A(512, "sgb")
    dma("sp", wsf_.ap.rearrange("p (g s) -> p g s", g=4), W["sg_w_s"][l].rearrange("g t s -> t g s"), writes=[wsf_])
    dma("sp", bbc.ap, W["sg_b_s"][l].rearrange("g t -> (g t)").partition_broadcast(128), writes=[bbc])
    wsb_ = A(256, "sgwb", BF16)
    cp(wsb_.ap, wsf_.ap, [wsf_], [wsb_])
    pt = PS[0]
    ptb_ = E.psb(0)
    for g in range(4):
        tr(pt, ptb_[:, g * 128:(g + 1) * 128], wsb_.ap[:, g * 128:(g + 1) * 128], E.identb, [wsb_, E.cb])
    wsT0 = A(256, "sgwT0", BF16)
    cp(wsT0.ap, ptb_[:, 0:512], [pt], [wsT0], eng="act")
    tt(wsT.ap, wsT0.ap, E.mask4Ib.ap, ALU.mult, [wsT0, E.mask4Ib], [wsT], eng="pool")
    vb = [A(256, "sgvb%d" % i, BF16) for i in range(2)]
    ub = [A(256, "sgub%d" % i, BF16) for i in range(2)]
    tf = [A(512, "sgtf%d" % i) for i in range(2)]
    ob = [A(256, "sgob%d" % i, BF16) for i in range(2)]
    for t in range(NT):
        i2 = t % 2
        sl = slice(t * 128, (t + 1) * 128)
        dma("sp", vb[i2].ap, SC["sgv"][sl, :], reads=dsl("sgv", t), writes=[vb[i2]])
        dma("sp", ub[i2].ap.rearrange("p (g t) -> p g t", g=4), SC["sgu"][:, sl].rearrange("(g p) t -> p g t", p=128),
            reads=dsl("sgu", t), writes=[ub[i2]])
        ps = PS[1 + i2]
        for g in range(4):
            gs_ = slice(g * 128, (g + 1) * 128)
            mm(ps, ps.ap[:, gs_], vb[i2].ap[:, gs_], wsT.ap[:, gs_], True, True, [vb[i2], wsT])
        tt(tf[i2].ap, ps.ap, bbc.ap, ALU.add, [ps, bbc], [tf[i2]])
        tt(ob[i2].ap, tf[i2].ap, ub[i2].ap, ALU.mult, [tf[i2], ub[i2]], [ob[i2]], eng="pool")
        dma("pool", SC["br"][1024:1536, sl].rearrange("(g p) t -> p g t", p=128), ob[i2].ap.rearrange("p (g t) -> p g t", g=4),
            reads=[ob[i2]], writes=dsl("br", t))


def phase_fox(nc, P, l, s, env):
    E = Env(env)
    W, SC, PS, NT = E.W, E.SC, E.PS, E.NT
    op, dma, mm, act, tt, ts, stt, cp, dsl = E.op, E.dma, E.mm, E.act, E.tt, E.ts, E.stt, E.cp, E.dsl
    A = P.alloc
    fb = A(8, "foxb")
    dma("sp", fb.ap[:, 0:4], W["fox_f_bias"][l].partition_broadcast(128), writes=[fb])
    Fcol = A(NT * 4, "Fcol")
    Fref = A(NT * 4, "Fref")
    carry = A(8, "carry")
    op("pool", lambda e: e.memset(carry.ap, 0.0), writes=[carry])
    gsb = [A(16, "fgs%d" % i) for i in range(2)]
    fpf = [A(24, "fpf%d" % i) for i in range(2)]
    fpb = [A(8, "fpb%d" % i, BF16) for i in range(2)]
    Fref3 = Fref.ap.rearrange("p (h t) -> p h t", h=4)
    for t in range(NT):
        g_ = gsb[t % 2]
        dma("sp", g_.ap[:, 0:12], SC["gsm"][t * 128:(t + 1) * 128, :], reads=dsl("gsm", t), writes=[g_])
        tt(g_.ap[:, 12:16], g_.ap[:, 8:12], fb.ap[:, 0:4], ALU.add, [g_, fb], [g_])
        act(g_.ap[:, 12:16], g_.ap[:, 12:16], AF.Exp, [g_], [g_], scale=-1.0)
        act(g_.ap[:, 12:16], g_.ap[:, 12:16], AF.Ln, [g_], [g_], bias=1.0)
        ts(g_.ap[:, 12:16], g_.ap[:, 12:16], -1.0, ALU.mult, [g_], [g_])
        pc, ptot = PS[6], PS[7]
        E.split_bf16(g_.ap[:, 12:16], g_, fpf[t % 2], fpb[t % 2], 3)
        for k in range(3):
            mm(pc, pc.ap[:, 0:4], E.maskIb, fpb[t % 2].ap[:, 4 * k:4 * k + 4], k == 0, k == 2, [E.cb, fpb[t % 2]])
        for k in range(3):
            mm(ptot, ptot.ap[:, 0:4], E.onesb, fpb[t % 2].ap[:, 4 * k:4 * k + 4], k == 0, k == 2, [E.cb, fpb[t % 2]])
        tt(Fcol.ap[:, t * 4:(t + 1) * 4], pc.ap[:, 0:4], carry.ap[:, 0:4], ALU.add, [pc, carry], [Fcol])
        tt(carry.ap[:, 0:4], carry.ap[:, 0:4], ptot.ap[:, 0:4], ALU.add, [carry, ptot], [carry])
        cp(Fref3[:, :, t], carry.ap[:, 0:4], [carry], [Fref])
    E.q_slots = dsl("fq", 0, NT)
    E.k_slots = dsl("fk", 0, NT)
    for h in range(4):
        def bias_fn(b_, i, h=h):
            ts(b_.ap[:, 0:NT], Fref3[:, h, :], Fcol.ap[:, i * 4 + h:i * 4 + h + 1], ALU.subtract, [Fref, Fcol], [b_])
        attention(nc, P, E, SC["fq"][h * 128:(h + 1) * 128, :], SC["fk"][h * 128:(h + 1) * 128, :], "fv", h,
                  [(0, 128)], bias_fn, 1536 + h * 128)


def phase_merge_ffn(nc, P, l, s, env):
    E = Env(env)
    W, SC, PS, NT, NB, S = E.W, E.SC, E.PS, E.NT, E.NB, E.S
    op, dma, mm, tr, act, tt, ts, stt, cp, dsl, psb = E.op, E.dma, E.mm, E.tr, E.act, E.tt, E.ts, E.stt, E.cp, E.dsl, E.psb
    A = P.alloc
    out, XS = E.out, E.XS
    x_src = E.x_in if l == 0 else out
    wbr = A(16 * 1024 // 2, "wbr", BF16)
    wbr3 = wbr.ap.rearrange("p (k n) -> p k n", k=16)
    wo = A(8 * 1024 // 2, "wo", BF16)
    wo3 = wo.ap.rearrange("p (k n) -> p k n", k=8)
    stg = [A(4096, "fstg%d" % i) for i in range(2)]
    for i in range(4):
        E.load_cast(wbr, wbr3[:, i * 4:(i + 1) * 4, :], W["w_branch"][l, i].rearrange("(k p) n -> p k n", p=128), 4096, stg)
    for hf in range(2):
        E.load_cast(wo, wo3[:, hf * 4:(hf + 1) * 4, :], W["w_out"][l, hf * 512:(hf + 1) * 512, :].rearrange("(k p) n -> p k n", p=128), 4096, stg)
    brb = [A(16 * 512 // 2, "brb%d" % i, BF16) for i in range(2)]
    gb = [A(4 * 512 // 2, "gb%d" % i, BF16) for i in range(2)]
    mf = [A(512, "mf%d" % i) for i in range(2)]
    tf = [A(512, "mtf%d" % i) for i in range(2)]
    mT = [A(8 * 512 // 2, "mT%d" % i, BF16) for i in range(2)]
    xt = [A(1024, "mx%d" % i) for i in range(2)]
    cnt = 0
    for b in range(NB):
        bs = slice(b * 512, (b + 1) * 512)
        br_ = brb[b % 2]
        br3 = br_.ap.rearrange("p (k t) -> p k t", k=16)
        for i4 in range(4):
            dma("sp", br3[:, i4 * 4:(i4 + 1) * 4, :], SC["br"][i4 * 512:(i4 + 1) * 512, bs].rearrange("(k p) t -> p k t", p=128),
                reads=dsl("br", b * 4, b * 4 + 4), writes=[br_])
        m_ = mT[b % 2]
        m3 = m_.ap.rearrange("p (k t) -> p k t", k=8)
        for c in range(8):
            g_ = gb[c % 2]
            g3 = g_.ap.rearrange("p (i t) -> p i t", i=4)
            dma("sp", g3, SC["gates"][:, bs].rearrange("(i c p) t -> c p i t", i=4, p=128)[c], reads=dsl("gates", b * 4, b * 4 + 4), writes=[g_])
            acc = mf[c % 2]
            for i in range(4):
                ps = PS[cnt % 4]
                cnt += 1
                for kc in range(4):
                    mm(ps, ps.ap, wbr3[:, i * 4 + kc, c * 128:(c + 1) * 128], br3[:, i * 4 + kc, :], kc == 0, kc == 3, [wbr, br_])
                if i == 0:
                    tt(acc.ap, ps.ap, g3[:, 0, :], ALU.mult, [ps, g_], [acc])
                else:
                    t_ = tf[i % 2]
                    tt(t_.ap, ps.ap, g3[:, i, :], ALU.mult, [ps, g_], [t_])
                    if i < 3:
                        tt(acc.ap, acc.ap, t_.ap, ALU.add, [acc, t_], [acc], eng="pool")
                    else:
                        tt(m3[:, c, :], acc.ap, t_.ap, ALU.add, [acc, t_], [m_], eng="pool")
        for tq in range(4):
            t = b * 4 + tq
            x_ = xt[t % 2]
            dma("sp", x_.ap, x_src[s, t * 128:(t + 1) * 128, :], reads=[XS[s][t]], writes=[x_])
            for hf in range(2):
                ps = PS[4 + (t * 2 + hf) % 4]
                for kc in range(8):
                    mm(ps, ps.ap, m3[:, kc, tq * 128:(tq + 1) * 128], wo3[:, kc, hf * 512:(hf + 1) * 512], kc == 0, kc == 7, [m_, wo])
                tt(x_.ap[:, hf * 512:(hf + 1) * 512], x_.ap[:, hf * 512:(hf + 1) * 512], ps.ap, ALU.add, [x_, ps], [x_])
            dma("pool", out[s, t * 128:(t + 1) * 128, :], x_.ap, reads=[x_], writes=[XS[s][t]])
    P.release(E.base_mark)
    P.pm('ffn')
    hT = A(8 * S // 2, "h2T", BF16)
    hT3 = hT.ap.rearrange("p (k t) -> p k t", k=8)
    hT_t = [Slot("h2T%d" % t) for t in range(NT)]
    for t_ in hT_t:
        t_.r = list(hT.r)
    m0_ = P.mark()
    g2bc = A(1024, "g2bc")
    dma("sp", g2bc.ap, W["norm2_g"][l].partition_broadcast(128), writes=[g2bc])
    xb = [A(1024, "fx%d" % i) for i in range(2)]
    hb = [A(512, "fh%d" % i, BF16) for i in range(2)]
    sm = [A(8, "fsm%d" % i) for i in range(2)]
    junk = A(512, "fjunk", BF16)
    for t in range(NT):
        x_, ht, st = xb[t % 2], hb[t % 2], sm[t % 2]
        dma("sp", x_.ap, out[s, t * 128:(t + 1) * 128, :], reads=[XS[s][t]], writes=[x_])
        act(junk.ap, x_.ap, AF.Square, [x_], [junk, st], accum=st.ap[:, 0:1])
        E.rstd_from(st, st.ap[:, 2:3], st.ap[:, 0:1], float(D), [st], st, st.ap[:, 1:2])
        stt(ht.ap, x_.ap, st.ap[:, 2:3], g2bc.ap, ALU.mult, ALU.mult, [x_, st, g2bc], [ht])
        pb, pbv = PS[t % 2], psb(t % 2)
        for k in range(8):
            tr(pb, pbv[:, k * 128:(k + 1) * 128], ht.ap[:, k * 128:(k + 1) * 128], E.identb, [ht, E.cb])
        c0 = t * 128
        op("act", lambda e, pbv=pbv, c0=c0: e.activation(out=hT3[:, :, c0:c0 + 128], in_=pbv.rearrange("p (k t) -> p k t", k=8),
                                                         func=AF.Identity), reads=[pb], writes=[hT_t[t]])
    P.release(m0_)
    m1 = P.mark()
    for half in range(2):
        P.release(m1)
        w1 = A(8 * 2048 // 2, "w1", BF16)
        w13 = w1.ap.rearrange("p (k n) -> p k n", k=8)
        w2 = A(16 * 1024 // 2, "w2", BF16)
        w23 = w2.ap.rearrange("p (k n) -> p k n", k=16)
        stg = [A(2048, "gstg%d" % i) for i in range(2)]
        for q in range(4):
            for kc2 in range(4):
                E.load_cast(w1, w13[:, kc2 * 2:kc2 * 2 + 2, q * 512:(q + 1) * 512],
                            W["w_ff1"][l, kc2 * 256:(kc2 + 1) * 256, half * 2048 + q * 512:half * 2048 + (q + 1) * 512].rearrange("(k p) n -> p k n", p=128),
                            1024, stg)
        for q in range(8):
            E.load_cast(w2, w23[:, q * 2:(q + 1) * 2, :],
                        W["w_ff2"][l, half * 2048 + q * 256:half * 2048 + (q + 1) * 256, :].rearrange("(k p) n -> p k n", p=128), 2048, stg)
        f1 = [A(16 * 512 // 2, "f1T%d" % i, BF16) for i in range(1)]
        rl = [A(512, "frl%d" % i) for i in range(2)]
        xo = [A(1024, "fxo%d" % i) for i in range(2)]
        cnt = 0
        for b in range(NB):
            f_ = f1[0]
            f3 = f_.ap.rearrange("p (k t) -> p k t", k=16)
            for fc in range(16):
                ps = PS[2 + cnt % 3]
                r_ = rl[cnt % 2]
                cnt += 1
                for kc in range(8):
                    mm(ps, ps.ap, w13[:, kc, fc * 128:(fc + 1) * 128], hT3[:, kc, b * 512:(b + 1) * 512], kc == 0, kc == 7,
                       [w1] + hT_t[b * 4:(b + 1) * 4])
                act(r_.ap, ps.ap, AF.Relu, [ps], [r_])
                tt(f3[:, fc, :], r_.ap, r_.ap, ALU.mult, [r_], [f_], eng=("dve" if fc % 2 else "pool"))
            for tq in range(4):
                t = b * 4 + tq
                x_ = xo[t % 2]
                dma("sp", x_.ap, out[s, t * 128:(t + 1) * 128, :], reads=[XS[s][t]], writes=[x_])
                for hf in range(2):
                    ps = PS[5 + (t * 2 + hf) % 3]
                    for kc in range(16):
                        mm(ps, ps.ap, f3[:, kc, tq * 128:(tq + 1) * 128], w23[:, kc, hf * 512:(hf + 1) * 512], kc == 0, kc == 15, [f_, w2])
                    tt(x_.ap[:, hf * 512:(hf + 1) * 512], x_.ap[:, hf * 512:(hf + 1) * 512], ps.ap, ALU.add, [x_, ps], [x_])
                dma("pool", out[s, t * 128:(t + 1) * 128, :], x_.ap, reads=[x_], writes=[XS[s][t]])
    fold(hT, hT_t)


_CACHE = {}


def kernel(**inputs):
    NCORE = 8
    x = np.asarray(inputs["x"], np.float32)
    B, S, _ = x.shape
    NSEQ = B // NCORE
    depth = int(np.asarray(inputs["w_in"]).shape[0])
    key = (NSEQ, S, depth)
    if key not in _CACHE:
        _CACHE[key] = build(NSEQ, S, depth)
    nc = _CACHE[key]
    consts = make_consts()
    shared = {k: np.ascontiguousarray(np.asarray(v, np.float32)) for k, v in inputs.items() if k not in ("x", "positions")}
    pos = np.asarray(inputs["positions"]).astype(np.int32)
    in_maps = []
    for c in range(NCORE):
        m = dict(shared)
        m["x"] = np.ascontiguousarray(x[c * NSEQ:(c + 1) * NSEQ])
        m["positions"] = np.ascontiguousarray(pos[c * NSEQ:(c + 1) * NSEQ])
        m["consts"] = consts
        in_maps.append(m)
    res = run_bass_kernel_spmd(nc, in_maps, core_ids=list(range(NCORE)))
    return np.concatenate([np.asarray(r["out"], np.float32) for r in res.results], axis=0)
```

```python
import math
import os
GSTOP = int(os.environ.get('GSTOP', '99'))
GVAR = int(os.environ.get('GVAR', '0'))
PHASES = os.environ.get('PHASES', 'pa,gdn,mla,sg,fox,mf').split(',')
DMAQ = {'pool': os.environ.get('STOREQ', 'pool')}
import numpy as np
import concourse.bass as bass
import concourse.mybir as mybir
from concourse.bass_utils import run_bass_kernel_spmd

F32 = mybir.dt.float32
BF16 = mybir.dt.bfloat16
I32 = mybir.dt.int32
AF = mybir.ActivationFunctionType
ALU = mybir.AluOpType

COMPUTE = ("pe", "act", "dve", "pool")
EPOCH = 30000
D = 1024
DIN = 9164
EPS = 1e-6


class Slot:
    __slots__ = ("w", "r", "name", "ap", "lo", "hi")

    def __init__(self, name="", ap=None, lo=0, hi=0):
        self.w = None
        self.r = []
        self.name = name
        self.ap = ap
        self.lo = lo
        self.hi = hi


class Rec:
    __slots__ = ("eng", "fn", "deps", "signal", "val", "is_dma", "sem")

    def __init__(self, eng, fn, deps, is_dma=False):
        self.eng = eng
        self.fn = fn
        self.deps = deps
        self.signal = False
        self.val = 0
        self.is_dma = is_dma
        self.sem = None


class Prog:
    def __init__(self, nc, arena_words=52992, ndma_sems=int(os.environ.get('NDMA', '6'))):
        self.nc = nc
        self.streams = {k: [] for k in ("pe", "act", "dve", "pool", "sp")}
        self.dma_pool = {q: [nc.alloc_semaphore("d_%s%d" % (q, i)) for i in range(ndma_sems)]
                         for q in ("sp", "pool", "act")}
        self.dma_last = {q: [None] * ndma_sems for q in self.dma_pool}
        self.dma_rr = {q: 0 for q in self.dma_pool}
        self.arena = nc.alloc_sbuf_tensor("arena", [128, arena_words], F32)
        self.arena_words = arena_words
        self.top = 0
        self.live = []
        self.phase_marks = []
        self.psum = []
        for i in range(8):
            t = nc.alloc_psum_tensor("ps%d" % i, [128, 512], F32)
            self.psum.append(Slot("ps%d" % i, t[:, :]))

    def pm(self, name):
        self.phase_marks.append((name, len(self.streams['pe'])))

    def mark(self):
        return self.top

    def release(self, mark):
        self.top = mark

    def alloc(self, words, name="", dtype=None):
        req = words
        words = (words + 7) // 8 * 8
        lo, hi = self.top, self.top + words
        assert hi <= self.arena_words, "arena overflow %s %d" % (name, hi)
        self.top = hi
        ap = self.arena[:, lo:lo + req]
        if dtype is BF16:
            ap = ap.bitcast(BF16)
        elif dtype is I32:
            ap = ap.bitcast(I32)
        s = Slot(name, ap, lo, hi)
        keep = []
        for o in self.live:
            if o.lo < hi and lo < o.hi:
                if o.w is not None:
                    s.r.append(o.w)
                s.r.extend(o.r)
                if o.lo < lo or o.hi > hi:
                    keep.append(o)
            else:
                keep.append(o)
        keep.append(s)
        self.live = keep
        return s

    def _deps(self, reads, writes):
        deps = []
        for s in reads:
            if s.w is not None:
                deps.append(s.w)
        for s in writes:
            if s.w is not None:
                deps.append(s.w)
            deps.extend(s.r)
        return deps

    def _update(self, rec, reads, writes):
        for s in writes:
            s.w = rec
            s.r = []
        for s in reads:
            if s.w is rec:
                continue
            if rec.is_dma:
                s.r.append(rec)
            else:
                s.r = [x for x in s.r if x.is_dma or x.eng != rec.eng]
                s.r.append(rec)

    def op(self, eng, fn, reads=(), writes=()):
        rec = Rec(eng, fn, self._deps(reads, writes))
        self._update(rec, reads, writes)
        self.streams[eng].append(rec)
        return rec

    def dma(self, q, out, in_, reads=(), writes=(), **kw):
        q = DMAQ.get(q, q)
        deps = self._deps(reads, writes)
        k = self.dma_rr[q]
        self.dma_rr[q] = (k + 1) % len(self.dma_pool[q])
        prev = self.dma_last[q][k]
        if prev is not None:
            deps.append(prev)
        rec = Rec(q, lambda e: e.dma_start(out=out, in_=in_, allow_slow_non_contiguous=True, **kw), deps, is_dma=True)
        rec.sem = self.dma_pool[q][k]
        rec.val = (prev.val if prev is not None else 0) + 16
        rec.signal = True
        self.dma_last[q][k] = rec
        self._update(rec, reads, writes)
        self.streams[q].append(rec)
        return rec

    def emit(self):
        nc = self.nc
        for q, st in self.streams.items():
            for rec in st:
                for d in rec.deps:
                    if not d.is_dma:
                        if d.eng == "pe" and rec.eng == "pe" and not rec.is_dma:
                            continue
                        d.signal = True
        finals = {q: [r for r in self.dma_last[q] if r is not None] for q in self.dma_pool}
        for q in COMPUTE:
            c = 0
            sem = None
            for rec in self.streams[q]:
                if rec.is_dma:
                    continue
                if rec.signal:
                    if sem is None or c >= EPOCH:
                        sem = nc.alloc_semaphore("s_%s_%d" % (q, len(self.streams[q]) + id(rec) % 100000))
                        c = 0
                    c += 1
                    rec.val = c
                    rec.sem = sem
        engs = {"pe": "tensor", "act": "scalar", "dve": "vector", "pool": "gpsimd", "sp": "sync"}
        with nc.Block() as block:
            def mk(q):
                def body(e):
                    waited = {}
                    for rec in self.streams[q]:
                        need = {}
                        for d in rec.deps:
                            if (not d.is_dma) and d.eng == "pe" and q == "pe" and not rec.is_dma:
                                continue
                            key = d.sem.num
                            if waited.get(key, 0) >= d.val:
                                continue
                            if key not in need or need[key][1] < d.val:
                                need[key] = (d.sem, d.val)
                        for key, (sem, val) in need.items():
                            e.wait_ge(sem, val)
                            waited[key] = val
                        ins = rec.fn(e)
                        if rec.is_dma:
                            ins.then_inc(rec.sem, 16)
                        elif rec.signal:
                            ins.then_inc(rec.sem, 1)
                    for r in finals.get(q, []):
                        if waited.get(r.sem.num, 0) < r.val:
                            e.wait_ge(r.sem, r.val)
                return body
            for q, nm in engs.items():
                getattr(block, nm)(mk(q))


C_ID, C_MI, C_MS, C_ONE, C_ROT, C_INVF, C_LVL, C_END = 0, 128, 256, 384, 512, 576, 584, 584 + 8 * 128


def make_consts():
    c = np.zeros((128, C_END), np.float32)
    j = np.arange(128)[:, None]
    i = np.arange(128)[None, :]
    c[:, C_ID:C_ID + 128] = (i == j)
    c[:, C_MI:C_MI + 128] = (i >= j)
    c[:, C_MS:C_MS + 128] = (i > j)
    c[:, C_ONE:C_ONE + 128] = 1.0
    R = np.zeros((64, 64), np.float32)
    for m in range(32):
        R[m + 32, m] = -1.0
        R[m, m + 32] = 1.0
    c[0:64, C_ROT:C_ROT + 64] = R
    invf = (10000.0 ** (-np.arange(0, 64, 2, dtype=np.float32) / 64)).astype(np.float32)
    c[0:32, C_INVF] = invf
    c[32:64, C_INVF] = invf
    p = np.arange(128)[:, None]
    f = np.arange(128)[None, :]
    for m in range(7):
        n = 1 << m
        mt = ((f // (2 * n)) == (p // (2 * n))) & ((f % (2 * n)) >= n) & ((p % (2 * n)) < n)
        c[:, C_LVL + m * 128:C_LVL + (m + 1) * 128] = mt
    c[:, C_LVL + 7 * 128:C_LVL + 8 * 128] = c[:, C_LVL:C_LVL + 128].T
    return c


def build(NSEQ, S, DEPTH, dbg=None):
    NT = S // 128
    NB = S // 512
    nc = bass.Bass("TRN2", target_bir_lowering=False)

    def din(name, shape, dt=F32):
        return nc.dram_tensor(name, list(shape), dt, kind="ExternalInput").ap()

    L = DEPTH
    x_in = din("x", [NSEQ, S, D])
    pos_in = din("positions", [NSEQ, S], I32)
    consts_in = din("consts", [128, C_END])
    W = {}
    for name, shape in [("norm1_g", [L, D]), ("w_in", [L, D, DIN]), ("dn_conv_w", [L, 4, 1536]),
                        ("dn_a_log", [L, 4]), ("dn_dt_bias", [L, 4]), ("dn_out_norm_g", [L, 128]),
                        ("mla_q_norm_g", [L, 256]), ("mla_kv_norm_g", [L, 128]),
                        ("mla_w_uq", [L, 256, 768]), ("mla_w_ukv", [L, 128, 1024]),
                        ("mla_qk_q_g", [L, 192]), ("mla_qk_k_g", [L, 192]),
                        ("sg_v_norm_g", [L, 4, 128]), ("sg_w_s", [L, 4, 128, 128]), ("sg_b_s", [L, 4, 128]),
                        ("fox_q_norm_g", [L, 128]), ("fox_k_norm_g", [L, 128]), ("fox_f_bias", [L, 4]),
                        ("w_branch", [L, 4, 512, D]), ("w_out", [L, D, D]), ("norm2_g", [L, D]),
                        ("w_ff1", [L, D, 4 * D]), ("w_ff2", [L, 4 * D, D])]:
        W[name] = din(name, shape)
    out = nc.dram_tensor("out", [NSEQ, S, D], F32, kind="ExternalOutput").ap()

    def scratch(name, shape, dt=BF16):
        return nc.dram_tensor(name, list(shape), dt, kind=("ExternalOutput" if dbg else "Internal")).ap()
    lvl = dbg if dbg else 99

    SC = {n: scratch(n, [r, S]) for n, r in
          [("dnq", 512), ("dnk", 512), ("dnv", 512), ("dnz", 512), ("cq", 256), ("ckv", 128), ("kr", 64),
           ("sgu", 512), ("fq", 512), ("fk", 512), ("gates", 4096), ("br", 2048), ("mq", 768), ("mk", 768)]}
    SC["gsm"] = scratch("gsm", [S, 12], F32)
    SC["sgv"] = scratch("sgv", [S, 512])
    SC["fv"] = scratch("fv", [S, 512])
    SC["mv"] = scratch("mv", [S, 512])
    DS = {n: [Slot("%s%d" % (n, t)) for t in range(NT)] for n in SC}
    XS = [[Slot("x%d_%d" % (s, t)) for t in range(NT)] for s in range(NSEQ)]

    def dsl(name, t0, t1=None):
        return DS[name][t0:(t1 if t1 is not None else t0 + 1)]

    P = Prog(nc)
    op, dma = P.op, P.dma
    PS = P.psum

    def psb(i):
        return PS[i].ap.bitcast(BF16)

    def mm(ps, out_ap, lhsT, rhs, start, stop, reads):
        op("pe", lambda e: e.matmul(out_ap, lhsT=lhsT, rhs=rhs, start=start, stop=stop), reads=reads, writes=[ps])

    def tr(ps, out_ap, in_ap, ident, reads):
        op("pe", lambda e: e.transpose(out=out_ap, in_=in_ap, identity=ident), reads=reads, writes=[ps])

    def act(out_ap, in_ap, func, reads, writes, bias=None, scale=None, accum=None, eng="act"):
        kw = {}
        if bias is not None:
            kw["bias"] = bias
        if scale is not None:
            kw["scale"] = scale
        if accum is not None:
            kw["accum_out"] = accum
        op(eng, lambda e: e.activation(out=out_ap, in_=in_ap, func=func, **kw), reads=reads, writes=writes)

    def tt(out_ap, a, b, alu, reads, writes, eng="dve"):
        op(eng, lambda e: e.tensor_tensor(out=out_ap, in0=a, in1=b, op=alu), reads=reads, writes=writes)

    def ts(out_ap, a, s1, alu, reads, writes, s2=None, alu2=None, eng="dve"):
        if alu2 is None:
            op(eng, lambda e: e.tensor_scalar(out=out_ap, in0=a, scalar1=s1, scalar2=None, op0=alu), reads=reads, writes=writes)
        else:
            op(eng, lambda e: e.tensor_scalar(out=out_ap, in0=a, scalar1=s1, scalar2=s2, op0=alu, op1=alu2), reads=reads, writes=writes)

    def stt(out_ap, a, s, b, alu0, alu1, reads, writes, eng="dve"):
        op(eng, lambda e: e.scalar_tensor_tensor(out=out_ap, in0=a, scalar=s, in1=b, op0=alu0, op1=alu1), reads=reads, writes=writes)

    def cp(out_ap, in_ap, reads, writes, eng="dve"):
        if eng == "act":
            op(eng, lambda e: e.activation(out=out_ap, in_=in_ap, func=AF.Identity), reads=reads, writes=writes)
        else:
            op(eng, lambda e: e.tensor_copy(out=out_ap, in_=in_ap), reads=reads, writes=writes)

    def recip(out_ap, in_ap, reads, writes):
        op("dve", lambda e: e.reciprocal(out=out_ap, in_=in_ap), reads=reads, writes=writes)

    def rstd_from(out_slot, out_ap, in_ap, n, reads, tmp_slot, tmp_ap):
        act(tmp_ap, in_ap, AF.Sqrt, list(reads) + [epsc], [tmp_slot], bias=epsc.ap[0:out_ap.shape[0], 0:1], scale=1.0 / n)
        recip(out_ap, tmp_ap, [tmp_slot], [out_slot])

    def split_bf16(x_ap, x_slot, pf, pb, npc):
        cur_ap, cur_slot = x_ap, x_slot
        for k in range(npc):
            cp(pb.ap[:, 4 * k:4 * k + 4], cur_ap, [cur_slot, pf], [pb])
            cp(pf.ap[:, 4 * k:4 * k + 4], pb.ap[:, 4 * k:4 * k + 4], [pb], [pf])
            if k < npc - 1:
                tt(pf.ap[:, 12 + 4 * k:16 + 4 * k], cur_ap, pf.ap[:, 4 * k:4 * k + 4], ALU.subtract, [cur_slot, pf], [pf])
                cur_ap, cur_slot = pf.ap[:, 12 + 4 * k:16 + 4 * k], pf

    cst = P.alloc(C_END, "cst")
    dma("sp", cst.ap, consts_in, writes=[cst])
    identf = cst.ap[:, C_ID:C_ID + 128]
    maskI = cst.ap[:, C_MI:C_MI + 128]
    maskS = cst.ap[:, C_MS:C_MS + 128]
    onesf = cst.ap[:, C_ONE:C_ONE + 128]
    cb = P.alloc(C_END // 2, "cb", BF16)
    cp(cb.ap, cst.ap, [cst], [cb])
    identb = cb.ap[:, C_ID:C_ID + 128]
    maskIb = cb.ap[:, C_MI:C_MI + 128]
    onesb = cb.ap[:, C_ONE:C_ONE + 128]
    rotb = cb.ap[0:64, C_ROT:C_ROT + 64]
    epsc = P.alloc(8, "epsc")
    op("pool", lambda e: e.memset(epsc.ap, EPS), writes=[epsc])
    mask4I = P.alloc(512, "mask4I")
    mask4S = P.alloc(512, "mask4S")
    ident4b = P.alloc(256, "ident4b", BF16)
    mask4Ib = P.alloc(256, "mask4Ib", BF16)
    for h in range(4):
        cp(mask4Ib.ap[:, h * 128:(h + 1) * 128], maskI, [cst], [mask4Ib], eng="pool")
        cp(mask4I.ap[:, h * 128:(h + 1) * 128], maskI, [cst], [mask4I], eng="pool")
        cp(mask4S.ap[:, h * 128:(h + 1) * 128], maskS, [cst], [mask4S], eng="pool")
        cp(ident4b.ap[:, h * 128:(h + 1) * 128], identf, [cst], [ident4b], eng="pool")
    lvl4 = P.alloc(8 * 256, "lvl4", BF16)
    for m in range(8):
        for h in range(4):
            cp(lvl4.ap[:, m * 512 + h * 128:m * 512 + (h + 1) * 128], cst.ap[:, C_LVL + m * 128:C_LVL + (m + 1) * 128], [cst], [lvl4], eng="pool")
    base_mark = P.mark()

    def col(vec_ap, n=128):
        return vec_ap.rearrange("(p o) -> p o", o=1)

    stage_i = [0]

    def load_cast(dst_slot, dst_ap, src_ap, shape_words, stg):
        k = stage_i[0] % len(stg)
        stage_i[0] += 1
        s = stg[k]
        sap = s.ap[:, 0:shape_words]
        if len(dst_ap.shape) == 3:
            sap = sap.rearrange("p (a b) -> p a b", a=dst_ap.shape[1])
        dma("sp", sap, src_ap, writes=[s])
        cp(dst_ap, sap, [s], [dst_slot], eng="pool")

    for l in range(L):
        for s in range(NSEQ):
            x_src = x_in if l == 0 else out
            P.release(base_mark)
            P.pm('pa0')
            hT = P.alloc(8 * S // 2, "hT", BF16)
            hT3 = hT.ap.rearrange("p (k t) -> p k t", k=8)
            hT_t = [Slot("hT%d" % t) for t in range(NT)]
            for t_ in hT_t:
                t_.r = list(hT.r)
            pa_mark = P.mark()
            g1bc = P.alloc(D, "g1bc")
            dma("sp", g1bc.ap, W["norm1_g"][l].partition_broadcast(128), writes=[g1bc])
            xb = [P.alloc(D, "xb%d" % i) for i in range(2)]
            hb = [P.alloc(D // 2, "hb%d" % i, BF16) for i in range(2)]
            sm = [P.alloc(8, "sm%d" % i) for i in range(2)]
            junk = P.alloc(D // 2, "junk", BF16)

            def norm_transpose(x_ap, xslots, gbc, dstT3, dst_slots_fn, ntiles, col0):
                for t in range(ntiles):
                    xt, ht, st = xb[t % 2], hb[t % 2], sm[t % 2]
                    dma("sp", xt.ap, x_ap(t), reads=xslots(t), writes=[xt])
                    act(junk.ap, xt.ap, AF.Square, [xt], [junk, st], accum=st.ap[:, 0:1])
                    rstd_from(st, st.ap[:, 2:3], st.ap[:, 0:1], float(D), [st], st, st.ap[:, 1:2])
                    stt(ht.ap, xt.ap, st.ap[:, 2:3], gbc.ap, ALU.mult, ALU.mult, [xt, st, gbc], [ht])
                    pb = PS[t % 2]
                    pbv = psb(t % 2)
                    for k in range(8):
                        tr(pb, pbv[:, k * 128:(k + 1) * 128], ht.ap[:, k * 128:(k + 1) * 128], identb, [ht, cb])
                    dsl_ = dst_slots_fn(t)
                    c0 = col0 + t * 128
                    op("act", lambda e, pbv=pbv, c0=c0: e.activation(
                        out=dstT3[:, :, c0:c0 + 128], in_=pbv.rearrange("p (k t) -> p k t", k=8), func=AF.Identity),
                       reads=[pb], writes=dsl_)

            norm_transpose(lambda t: x_src[s, t * 128:(t + 1) * 128, :], lambda t: [XS[s][t]], g1bc, hT3,
                           lambda t: [hT_t[t]], NT, 0)
            P.release(pa_mark)
            if lvl < 2:
                continue
            if 'pa' not in PHASES:
                fm_skip = True
            else:
                fm_skip = False
            P.pm('pa1')
            wsf = [P.alloc(8 * 128, "wsf%d" % i) for i in range(2)]
            wsb = [P.alloc(8 * 128 // 2, "wsb%d" % i, BF16) for i in range(2)]
            ob = [P.alloc(256, "ob%d" % i, BF16) for i in range(3)]
            sqb = [P.alloc(256, "sqb%d" % i, BF16) for i in range(2)]
            rsf = [P.alloc(512, "rsf%d" % i) for i in range(2)]
            rtf = [P.alloc(512, "rtf%d" % i) for i in range(2)]
            svf = [P.alloc(512, "svf%d" % i) for i in range(2)]
            xin = [P.alloc(520, "xin%d" % i) for i in range(2)]
            accf = [P.alloc(512, "accf%d" % i) for i in range(2)]
            gcols = P.alloc(16, "gcols")
            convw = P.alloc(48, "convw")
            with nc.allow_non_contiguous_dma(reason="small param layouts"):
                for kk in range(4):
                    dma("sp", convw.ap[:, kk * 12:(kk + 1) * 12], W["dn_conv_w"][l, kk].rearrange("(g p) -> p g", p=128), writes=[convw])
                dma("sp", gcols.ap[:, 0:1], col(W["fox_q_norm_g"][l]), writes=[gcols])
                dma("sp", gcols.ap[:, 1:2], col(W["fox_k_norm_g"][l]), writes=[gcols])
                dma("sp", gcols.ap[:, 2:4], W["mla_q_norm_g"][l].rearrange("(c p) -> p c", p=128), writes=[gcols])
                dma("sp", gcols.ap[:, 4:5], col(W["mla_kv_norm_g"][l]), writes=[gcols])
            ts(gcols.ap[:, 0:1], gcols.ap[:, 0:1], 128.0 ** -0.5, ALU.mult, [gcols], [gcols])

            gi = [0]
            ei = [0]

            pending = []

            def fm_group(c0, ncol, kind, dst, r0, extra=None):
                if fm_skip:
                    return
                pending.append((c0, ncol, kind, dst, r0, extra))

            def fm_load(c0, ncol, kind, dst, r0, extra):
                k = gi[0] % 2
                gi[0] += 1
                wf, wb = wsf[k], wsb[k]
                wf3 = wf.ap[:, 0:8 * ncol].rearrange("p (k n) -> p k n", k=8)
                wb3 = wb.ap[:, 0:8 * ncol].rearrange("p (k n) -> p k n", k=8)
                for hh in range(2):
                    dma("sp", wf3[:, hh * 4:(hh + 1) * 4, :], W["w_in"][l, hh * 512:(hh + 1) * 512, c0:c0 + ncol].rearrange("(kc kp) n -> kp kc n", kp=128), writes=[wf])
                cp(wb3, wf3, [wf], [wb], eng="pool")
                return wb, wb3

            def fm_flush():
                if not pending:
                    return
                loaded = fm_load(*pending[0])
                for n_, a_ in enumerate(pending):
                    nxt = fm_load(*pending[n_ + 1]) if n_ + 1 < len(pending) else None
                    fm_compute(a_, loaded)
                    loaded = nxt
                del pending[:]

            def fm_compute(a_, loaded):
                c0, ncol, kind, dst, r0, extra = a_
                wb, wb3 = loaded
                for b in range(NB):
                    e_ = ei[0]
                    ei[0] += 1
                    ps = PS[2 + e_ % 3]
                    pa = ps.ap[0:ncol, :]
                    for kc in range(8):
                        mm(ps, pa, wb3[:, kc, :], hT3[:, kc, b * 512:(b + 1) * 512], kc == 0, kc == 7,
                           [wb] + hT_t[b * 4:(b + 1) * 4])
                    o = ob[e_ % 3]
                    oa = o.ap[0:ncol, :]
                    dd = SC[dst][r0:r0 + ncol, b * 512:(b + 1) * 512]
                    dsl_ = dsl(dst, b * 4, b * 4 + 4)
                    if kind in ("silu", "gelu", "sigmoid", "copy"):
                        f = {"silu": AF.Silu, "gelu": AF.Gelu_apprx_tanh, "sigmoid": AF.Sigmoid, "copy": AF.Identity}[kind]
                        act(oa, pa, f, [ps], [o])
                    elif kind == "rms":
                        gc_ap, nfeat = extra
                        sq, rs, rt = sqb[e_ % 2], rsf[e_ % 2], rtf[e_ % 2]
                        act(sq.ap[0:ncol, :], pa, AF.Square, [ps], [sq])
                        p2 = PS[5 + e_ % 2]
                        mm(p2, p2.ap, onesb[0:ncol, :], sq.ap[0:ncol, :], True, True, [sq, cb])
                        rstd_from(rs, rs.ap, p2.ap, float(nfeat), [p2], rt, rt.ap)
                        stt(oa, pa, gc_ap, rs.ap[0:ncol, :], ALU.mult, ALU.mult, [ps, rs, gcols], [o])
                    elif kind == "conv":
                        g_idx, which = extra
                        xi, xp = xin[b % 2], xin[(b + 1) % 2]
                        act(xi.ap[:, 3:515], pa, AF.Identity, [ps], [xi])
                        if b == 0:
                            op("pool", lambda e, xi=xi: e.memset(xi.ap[:, 0:3], 0.0), writes=[xi])
                        else:
                            cp(xi.ap[:, 0:3], xp.ap[:, 512:515], [xp], [xi], eng="pool")
                        ac = accf[e_ % 2]
                        wk = [convw.ap[:, kk * 12 + g_idx:kk * 12 + g_idx + 1] for kk in range(4)]
                        ts(ac.ap, xi.ap[:, 0:512], wk[0], ALU.mult, [xi, convw], [ac])
                        for kk in range(1, 4):
                            stt(ac.ap, xi.ap[:, kk:kk + 512], wk[kk], ac.ap, ALU.mult, ALU.add, [xi, convw, ac], [ac])
                        if which == "v":
                            act(oa, ac.ap, AF.Silu, [ac], [o])
                        else:
                            sv = svf[e_ % 2]
                            act(sv.ap, ac.ap, AF.Silu, [ac], [sv])
                            sq, rs, rt = sqb[e_ % 2], rsf[e_ % 2], rtf[e_ % 2]
                            act(sq.ap, sv.ap, AF.Square, [sv], [sq])
                            p2 = PS[5 + e_ % 2]
                            mm(p2, p2.ap, onesb, sq.ap, True, True, [sq, cb])
                            rstd_from(rs, rs.ap, p2.ap, 1.0, [p2], rt, rt.ap)
                            cs = (128.0 ** -0.5) if which == "q" else 1.0
                            stt(oa, sv.ap, cs, rs.ap, ALU.mult, ALU.mult, [sv, rs], [o])
                    dma("pool", dd, oa, reads=[o], writes=dsl_)

            for g in range(12):
                which = "qkv"[g // 4]
                fm_group(g * 128, 128, "conv", {"q": "dnq", "k": "dnk", "v": "dnv"}[which], (g % 4) * 128, (g, which))
            for g in range(4):
                fm_group(1536 + g * 128, 128, "silu", "dnz", g * 128)
            for b in range(NB):
                pass
            fm_flush()
            wf, wb = wsf[gi[0] % 2], wsb[gi[0] % 2]
            gi[0] += 1
            wf2 = [wf, wsf[gi[0] % 2]]
            wb2 = [wb, wsb[gi[0] % 2]]
            gi[0] += 1
            w3 = []
            for c in range(2):
                wf3 = wf2[c].ap.rearrange("p (k n) -> p k n", k=8)
                wb3 = wb2[c].ap.rearrange("p (k n) -> p k n", k=8)
                for hh in range(2):
                    dma("sp", wf3[:, hh * 4:(hh + 1) * 4, :], W["w_in"][l, hh * 512:(hh + 1) * 512, 2056 + c * 128:2056 + (c + 1) * 128].rearrange("(kc kp) n -> kp kc n", kp=128), writes=[wf2[c]])
                cp(wb3, wf3, [wf2[c]], [wb2[c]], eng="pool")
                w3.append(wb3)
            for b in range(NB):
                pss = [PS[2], PS[3]]
                for c in range(2):
                    for kc in range(8):
                        mm(pss[c], pss[c].ap, w3[c][:, kc, :], hT3[:, kc, b * 512:(b + 1) * 512], kc == 0, kc == 7,
                           [wb2[c]] + hT_t[b * 4:(b + 1) * 4])
                p2 = PS[5]
                for c in range(2):
                    act(sqb[c].ap, pss[c].ap, AF.Square, [pss[c]], [sqb[c]])
                    mm(p2, p2.ap, onesb, sqb[c].ap, c == 0, c == 1, [sqb[c], cb])
                rstd_from(rsf[0], rsf[0].ap, p2.ap, 256.0, [p2], rtf[0], rtf[0].ap)
                for c in range(2):
                    stt(ob[c].ap, pss[c].ap, gcols.ap[:, 2 + c:3 + c], rsf[0].ap, ALU.mult, ALU.mult, [pss[c], rsf[0], gcols], [ob[c]])
                    dma("pool", SC["cq"][c * 128:(c + 1) * 128, b * 512:(b + 1) * 512], ob[c].ap, reads=[ob[c]],
                        writes=dsl("cq", b * 4, b * 4 + 4))
            fm_group(2312, 128, "rms", "ckv", 0, (gcols.ap[:, 4:5], 128))
            fm_group(2440, 64, "copy", "kr", 0)
            for g in range(4):
                fm_group(2504 + g * 128, 128, "gelu", "sgu", g * 128)
            for g in range(4):
                fm_group(3528 + g * 128, 128, "rms", "fq", g * 128, (gcols.ap[:, 0:1], 128))
            for g in range(4):
                fm_group(3528 + 512 + g * 128, 128, "rms", "fk", g * 128, (gcols.ap[:, 1:2], 128))
            for g in range(32):
                fm_group(5068 + g * 128, 128, "sigmoid", "gates", g * 128)
            fm_flush()
            if lvl < 3:
                continue
            P.pm('pa2')
            m2 = P.mark()
            wtf = P.alloc(8 * 512, "wtf")
            wtb = P.alloc(8 * 512 // 2, "wtb", BF16)
            tob = [P.alloc(256, "tob%d" % i, BF16) for i in range(2)]
            tof = [P.alloc(512, "tof%d" % i) for i in range(2)]
            sgg = P.alloc(512, "sgg")
            dma("sp", sgg.ap, W["sg_v_norm_g"][l].rearrange("g c -> (g c)").partition_broadcast(128), writes=[sgg])
            ss4 = [P.alloc(16, "ss4%d" % i) for i in range(2)]

            def tm_group(cols, kind, dst):
                if fm_skip:
                    return
                ncol = sum(n for _, n in cols)
                wf3 = wtf.ap[:, 0:8 * ncol].rearrange("p (k n) -> p k n", k=8)
                wb3 = wtb.ap[:, 0:8 * ncol].rearrange("p (k n) -> p k n", k=8)
                o0 = 0
                with nc.allow_non_contiguous_dma(reason="narrow weight columns"):
                    for c0, n in cols:
                        for hh in range(2):
                            dma("sp", wf3[:, hh * 4:(hh + 1) * 4, o0:o0 + n], W["w_in"][l, hh * 512:(hh + 1) * 512, c0:c0 + n].rearrange("(kc kp) n -> kp kc n", kp=128), writes=[wtf])
                        o0 += n
                cp(wb3, wf3, [wtf], [wtb], eng="pool")
                for t in range(NT):
                    ps = PS[2 + t % 3]
                    pa = ps.ap[:, 0:ncol]
                    for kc in range(8):
                        mm(ps, pa, hT3[:, kc, t * 128:(t + 1) * 128], wb3[:, kc, :], kc == 0, kc == 7, [wtb, hT_t[t]])
                    if kind == "copyb":
                        o = tob[t % 2]
                        act(o.ap, pa, AF.Identity, [ps], [o])
                        dma("pool", SC[dst][t * 128:(t + 1) * 128, :], o.ap, reads=[o], writes=dsl(dst, t))
                    elif kind == "copyf":
                        o = tof[t % 2]
                        act(o.ap[:, 0:ncol], pa, AF.Identity, [ps], [o])
                        dma("pool", SC[dst][t * 128:(t + 1) * 128, :], o.ap[:, 0:ncol], reads=[o], writes=dsl(dst, t))
                    elif kind == "sgv":
                        gv, s4, o = tof[t % 2], ss4[t % 2], tob[t % 2]
                        act(gv.ap, pa, AF.Gelu_apprx_tanh, [ps], [gv])
                        for g in range(4):
                            act(junk.ap[:, 0:128], gv.ap[:, g * 128:(g + 1) * 128], AF.Square, [gv], [junk, s4], accum=s4.ap[:, g:g + 1])
                        rstd_from(s4, s4.ap[:, 8:12], s4.ap[:, 0:4], 128.0, [s4], s4, s4.ap[:, 4:8])
                        for g in range(4):
                            stt(o.ap[:, g * 128:(g + 1) * 128], gv.ap[:, g * 128:(g + 1) * 128], s4.ap[:, 8 + g:9 + g],
                                sgg.ap[:, g * 128:(g + 1) * 128], ALU.mult, ALU.mult, [gv, s4, sgg], [o])
                        dma("pool", SC[dst][t * 128:(t + 1) * 128, :], o.ap, reads=[o], writes=dsl(dst, t))

            tm_group([(3016, 512)], "sgv", "sgv")
            tm_group([(3528 + 1024, 512)], "copyb", "fv")
            tm_group([(2048, 8), (5064, 4)], "copyf", "gsm")
            P.release(base_mark)

            if lvl < 4:
                continue
            P.pm('gdn')
            if 'gdn' in PHASES:
                phase_gdn(nc, P, l, s, locals())
            else:
                fold(hT, hT_t)
            P.release(base_mark)
            if lvl < 5:
                continue
            P.pm('mla')
            if 'mla' in PHASES:
                phase_mla(nc, P, l, s, locals())
            P.release(base_mark)
            if lvl < 6:
                continue
            P.pm('sg')
            if 'sg' in PHASES:
                phase_sg(nc, P, l, s, locals())
            P.release(base_mark)
            if lvl < 7:
                continue
            P.pm('fox')
            if 'fox' in PHASES:
                phase_fox(nc, P, l, s, locals())
            P.release(base_mark)
            if lvl < 8:
                continue
            P.pm('merge')
            if 'mf' in PHASES:
                phase_merge_ffn(nc, P, l, s, locals())
    P.pm('end')
    P.emit()
    nc._phase_marks = P.phase_marks
    return nc


def fold(parent, children):
    for c in children:
        if c.w is not None:
            parent.r.append(c.w)
        parent.r.extend(c.r)


class Env:
    def __init__(self, d):
        self.__dict__.update(d)


def phase_gdn(nc, P, l, s, env):
    E = Env(env)
    fold(E.hT, E.hT_t)
    W, SC, PS, NT = E.W, E.SC, E.PS, E.NT
    op, dma, mm, tr, act, tt, ts, stt, cp, recip, dsl, psb = E.op, E.dma, E.mm, E.tr, E.act, E.tt, E.ts, E.stt, E.cp, E.recip, E.dsl, E.psb
    cst, cb = E.cst, E.cb
    A = P.alloc
    prm = A(32, "gprm")
    dma("sp", prm.ap[:, 0:4], W["dn_dt_bias"][l].partition_broadcast(128), writes=[prm])
    dma("sp", prm.ap[:, 4:8], W["dn_a_log"][l].partition_broadcast(128), writes=[prm])
    with nc.allow_non_contiguous_dma(reason="col"):
        dma("sp", prm.ap[:, 8:9], E.col(W["dn_out_norm_g"][l]), writes=[prm])
    act(prm.ap[:, 4:8], prm.ap[:, 4:8], AF.Exp, [prm], [prm])
    ts(prm.ap[:, 4:8], prm.ap[:, 4:8], -1.0, ALU.mult, [prm], [prm])
    Sf = A(512, "Sf")
    Sb = A(256, "Sb", BF16)
    op("pool", lambda e: e.memset(Sf.ap, 0.0), writes=[Sf])
    op("pool", lambda e: e.memset(Sb.ap, 0.0), writes=[Sb])

    def buf2(words, name, dt=None):
        return [A(words, "%s%d" % (name, i), dt) for i in range(2)]
    qT, kT, vT, zs = buf2(256, "gq", BF16), buf2(256, "gk", BF16), buf2(256, "gv", BF16), buf2(256, "gz", BF16)
    gs = buf2(32, "gs")
    rgb = buf2(512, "rgb", BF16)
    gpf = buf2(24, "gpf")
    gpb = buf2(8, "gpb", BF16)
    ET = buf2(512, "ET")
    DT = buf2(512, "DT")
    DTs = buf2(512, "DTs")
    EG = buf2(512, "EG")
    aqk = buf2(256, "aqk", BF16)
    Xb = buf2(256, "Xb", BF16)
    Pa = buf2(256, "Pa", BF16)
    PaT = buf2(256, "PaT", BF16)
    TT = buf2(256, "TT", BF16)
    Tb = buf2(256, "Tb", BF16)
    Mb = A(256, "Mb", BF16)
    Mb2 = A(256, "Mb2", BF16)
    Cm = [A(256, "Cm%d" % i, BF16) for i in range(6)]
    rhu = buf2(256, "rhu", BF16)
    rhw = buf2(256, "rhw", BF16)
    kde = buf2(256, "kde", BF16)
    uf = buf2(512, "uf")
    wT = buf2(256, "wT", BF16)
    qd = buf2(256, "qd", BF16)
    vn = buf2(256, "vn", BF16)
    sqo = buf2(256, "sqo", BF16)
    rso = buf2(512, "rso")
    rto = buf2(512, "rto")
    of = buf2(512, "of")
    ob_ = buf2(256, "gob", BF16)

    def h4(ap):
        return ap.rearrange("p (h t) -> p h t", h=4)

    for t in range(NT):
        i2 = t % 2
        sl = slice(t * 128, (t + 1) * 128)
        for buf, nm in ((qT, "dnq"), (kT, "dnk"), (vT, "dnv"), (zs, "dnz")):
            dma("sp", h4(buf[i2].ap), SC[nm][:, sl].rearrange("(h p) t -> p h t", p=128), reads=dsl(nm, t), writes=[buf[i2]])
        g_ = gs[i2]
        dma("sp", g_.ap[:, 0:12], SC["gsm"][sl, :], reads=dsl("gsm", t), writes=[g_])
        tt(g_.ap[:, 12:16], g_.ap[:, 0:4], prm.ap[:, 0:4], ALU.add, [g_, prm], [g_])
        act(g_.ap[:, 12:16], g_.ap[:, 12:16], AF.Exp, [g_], [g_])
        act(g_.ap[:, 12:16], g_.ap[:, 12:16], AF.Ln, [g_], [g_], bias=1.0)
        tt(g_.ap[:, 12:16], g_.ap[:, 12:16], prm.ap[:, 4:8], ALU.mult, [g_, prm], [g_])
        act(g_.ap[:, 16:20], g_.ap[:, 4:8], AF.Sigmoid, [g_], [g_])
        pf_, pb_ = gpf[i2], gpb[i2]
        E.split_bf16(g_.ap[:, 12:16], g_, pf_, pb_, 3)
        pg = PS[0]
        for k in range(3):
            mm(pg, pg.ap[:, 0:4], E.maskIb, pb_.ap[:, 4 * k:4 * k + 4], k == 0, k == 2, [cb, pb_])
        ts(g_.ap[:, 20:24], pg.ap[:, 0:4], -1.0, ALU.mult, [pg], [g_])
        pr = PS[1]
        for h in range(4):
            for k in range(2):
                rgk = rgb[i2].ap[:, (h * 2 + k) * 128:(h * 2 + k + 1) * 128]
                ts(rgk, E.maskIb, pf_.ap[:, 4 * k + h:4 * k + h + 1], ALU.mult, [cb, pf_], [rgb[i2]], eng=("dve" if k == 0 else "pool"))
                mm(pr, pr.ap[:, h * 128:(h + 1) * 128], E.onesb, rgk, k == 0, k == 1, [cb, rgb[i2]])
        cp(g_.ap[:, 24:28], h4(pr.ap)[:, :, 127], [pr], [g_])
        tt(g_.ap[:, 28:32], g_.ap[:, 24:28], g_.ap[:, 20:24], ALU.add, [g_], [g_])
        for h in range(4):
            ts(ET[i2].ap[:, h * 128:(h + 1) * 128], pr.ap[:, h * 128:(h + 1) * 128], g_.ap[:, 20 + h:21 + h], ALU.add, [pr, g_], [ET[i2]],
               s2=0.0, alu2=ALU.min)
        act(ET[i2].ap, ET[i2].ap, AF.Exp, [ET[i2]], [ET[i2]])
        act(EG[i2].ap, pr.ap, AF.Exp, [pr], [EG[i2]])
        tt(DT[i2].ap, ET[i2].ap, E.mask4I.ap, ALU.mult, [ET[i2], E.mask4I], [DT[i2]])
        tt(DTs[i2].ap, DT[i2].ap, E.mask4S.ap, ALU.mult, [DT[i2], E.mask4S], [DTs[i2]], eng="pool")
        g2 = gs[i2]
        act(g_.ap[:, 28:32], g_.ap[:, 28:32], AF.Exp, [g_], [g_])
        act(g_.ap[:, 24:28], g_.ap[:, 24:28], AF.Exp, [g_], [g_])
        act(g_.ap[:, 0:4], g_.ap[:, 20:24], AF.Exp, [g_], [g_], scale=-1.0)
        tt(g_.ap[:, 0:4], g_.ap[:, 0:4], g_.ap[:, 16:20], ALU.mult, [g_], [g_])
        ts(g_.ap[:, 4:8], g_.ap[:, 16:20], -1.0, ALU.mult, [g_], [g_])
        if GSTOP <= 1:
            continue
        pq, pk = PS[2], PS[3]
        q3, k3, v3 = h4(qT[i2].ap), h4(kT[i2].ap), h4(vT[i2].ap)
        for h in range(4):
            mm(pq, pq.ap[:, h * 128:(h + 1) * 128], k3[:, h, :], q3[:, h, :], True, True, [kT[i2], qT[i2]])
            mm(pk, pk.ap[:, h * 128:(h + 1) * 128], k3[:, h, :], k3[:, h, :], True, True, [kT[i2]])
        tt(aqk[i2].ap, pq.ap, DT[i2].ap, ALU.mult, [pq, DT[i2]], [aqk[i2]])
        tt(Xb[i2].ap, pk.ap, DTs[i2].ap, ALU.mult, [pk, DTs[i2]], [Xb[i2]])
        if GSTOP <= 2:
            continue
        pt = PS[4]
        ptb = psb(4)
        for h in range(4):
            tr(pt, ptb[:, h * 128:(h + 1) * 128], Xb[i2].ap[:, h * 128:(h + 1) * 128], E.identb, [Xb[i2], cb])
        cur, curT, curTT = Pa[0], PaT[0], TT[0]
        for h in range(4):
            ts(cur.ap[:, h * 128:(h + 1) * 128], ptb[:, h * 128:(h + 1) * 128], g_.ap[:, 4 + h:5 + h], ALU.mult, [pt, g_], [cur])
        pt2 = PS[5]
        pt2b = psb(5)
        if GVAR == 2:
            continue
        for h in range(4):
            tr(pt2, pt2b[:, h * 128:(h + 1) * 128], cur.ap[:, h * 128:(h + 1) * 128], E.identb, [cur, cb])
        if GVAR == 1:
            continue
        cp(curT.ap, pt2b[:, 0:512], [pt2], [curT], eng="act")
        if GVAR == 3:
            continue
        if GSTOP <= 3:
            continue
        pv, pkk = PS[6], PS[7]
        pvb, pkb = psb(6), psb(7)
        for h in range(4):
            tr(pv, pvb[:, h * 128:(h + 1) * 128], v3[:, h, :], E.identb, [vT[i2], cb])
            tr(pkk, pkb[:, h * 128:(h + 1) * 128], k3[:, h, :], E.identb, [kT[i2], cb])
        for h in range(4):
            hs = slice(h * 128, (h + 1) * 128)
            ts(rhu[i2].ap[:, hs], pvb[:, hs], g_.ap[:, 16 + h:17 + h], ALU.mult, [pv, g_], [rhu[i2]])
            ts(rhw[i2].ap[:, hs], pkb[:, hs], g_.ap[:, 0 + h:1 + h], ALU.mult, [pkk, g_], [rhw[i2]])
            ts(kde[i2].ap[:, hs], pkb[:, hs], g_.ap[:, 28 + h:29 + h], ALU.mult, [pkk, g_], [kde[i2]])
        if GSTOP <= 4:
            continue
        def lv(m):
            return E.lvl4.ap[:, m * 512:(m + 1) * 512]
        Tc, TTc = Tb[0], TT[1]
        tt(Mb.ap, cur.ap, lv(7), ALU.mult, [cur, E.lvl4], [Mb], eng="pool")
        tt(Tc.ap, Mb.ap, E.ident4b.ap, ALU.add, [Mb, E.ident4b], [Tc], eng="pool")
        tt(Mb2.ap, curT.ap, lv(0), ALU.mult, [curT, E.lvl4], [Mb2])
        tt(TTc.ap, Mb2.ap, E.ident4b.ap, ALU.add, [Mb2, E.ident4b], [TTc])
        for m in range(1, 7):
            tt(Cm[m - 1].ap, curT.ap, lv(m), ALU.mult, [curT, E.lvl4], [Cm[m - 1]], eng="pool")
        for m in range(1, 7):
            pM, pU, pV = PS[4], PS[5], PS[6]
            for h in range(4):
                hs = slice(h * 128, (h + 1) * 128)
                mm(pM, pM.ap[:, hs], Cm[m - 1].ap[:, hs], Tc.ap[:, hs], True, True, [Cm[m - 1], Tc])
            cp(Mb.ap, pM.ap, [pM], [Mb], eng="act")
            Tn, TTn = Tb[m % 2], TT[m % 2]
            for h in range(4):
                hs = slice(h * 128, (h + 1) * 128)
                if m < 6:
                    mm(pU, pU.ap[:, hs], E.identb, Tc.ap[:, hs], True, False, [cb, Tc])
                    mm(pU, pU.ap[:, hs], TTc.ap[:, hs], Mb.ap[:, hs], False, True, [TTc, Mb])
                mm(pV, pV.ap[:, hs], E.identb, TTc.ap[:, hs], True, False, [cb, TTc])
                mm(pV, pV.ap[:, hs], Mb.ap[:, hs], TTc.ap[:, hs], False, True, [TTc, Mb])
            if m < 6:
                cp(Tn.ap, pU.ap, [pU], [Tn], eng="act")
            cp(TTn.ap, pV.ap, [pV], [TTn])
            Tc, TTc = Tn, TTn
        curTT = TTc
        pu, pw = PS[4], PS[5]
        for h in range(4):
            hs = slice(h * 128, (h + 1) * 128)
            mm(pu, pu.ap[:, hs], curTT.ap[:, hs], rhu[i2].ap[:, hs], True, True, [curTT, rhu[i2]])
            mm(pw, pw.ap[:, hs], rhw[i2].ap[:, hs], curTT.ap[:, hs], True, True, [curTT, rhw[i2]])
        cp(uf[i2].ap, pu.ap, [pu], [uf[i2]], eng="act")
        cp(wT[i2].ap, pw.ap, [pw], [wT[i2]])
        tt(qd[i2].ap, qT[i2].ap, EG[i2].ap, ALU.mult, [qT[i2], EG[i2]], [qd[i2]], eng="pool")
        if GSTOP <= 6:
            continue
        p1, p2, p3 = PS[6], PS[7], PS[0]
        for h in range(4):
            hs = slice(h * 128, (h + 1) * 128)
            mm(p1, p1.ap[:, hs], wT[i2].ap[:, hs], Sb.ap[:, hs], True, True, [wT[i2], Sb])
        tt(vn[i2].ap, uf[i2].ap, p1.ap, ALU.subtract, [uf[i2], p1], [vn[i2]])
        for h in range(4):
            hs = slice(h * 128, (h + 1) * 128)
            mm(p2, p2.ap[:, hs], Sb.ap[:, hs], qd[i2].ap[:, hs], True, False, [Sb, qd[i2]])
            mm(p2, p2.ap[:, hs], vn[i2].ap[:, hs], aqk[i2].ap[:, hs], False, True, [vn[i2], aqk[i2]])
            mm(p3, p3.ap[:, hs], kde[i2].ap[:, hs], vn[i2].ap[:, hs], True, True, [kde[i2], vn[i2]])
        for h in range(4):
            hs = slice(h * 128, (h + 1) * 128)
            stt(Sf.ap[:, hs], Sf.ap[:, hs], g_.ap[:, 24 + h:25 + h], p3.ap[:, hs], ALU.mult, ALU.add, [Sf, g_, p3], [Sf])
        cp(Sb.ap, Sf.ap, [Sf], [Sb], eng="act")
        if GSTOP <= 7:
            continue
        act(sqo[i2].ap, p2.ap, AF.Square, [p2], [sqo[i2]])
        po = PS[1]
        mm(po, po.ap, E.onesb, sqo[i2].ap, True, True, [cb, sqo[i2]])
        E.rstd_from(rso[i2], rso[i2].ap, po.ap, 128.0, [po], rto[i2], rto[i2].ap)
        tt(of[i2].ap, p2.ap, rso[i2].ap, ALU.mult, [p2, rso[i2]], [of[i2]])
        stt(ob_[i2].ap, of[i2].ap, prm.ap[:, 8:9], zs[i2].ap, ALU.mult, ALU.mult, [of[i2], prm, zs[i2]], [ob_[i2]])
        dma("pool", SC["br"][0:512, sl].rearrange("(h p) t -> p h t", p=128), h4(ob_[i2].ap), reads=[ob_[i2]], writes=dsl("br", t))


def attention(nc, P, E, qsrc, ksrc, vname, h, kchunks, bias_fn, br_row0):
    SC, PS, NT, NB, S = E.SC, E.PS, E.NT, E.NB, E.S
    op, dma, mm, act, tt, cp, recip, dsl = E.op, E.dma, E.mm, E.act, E.tt, E.cp, E.recip, E.dsl
    A = P.alloc
    m0 = P.mark()
    qs, ks = [], []
    for ci, (r0, nr) in enumerate(kchunks):
        q_ = A(S // 2, "aq%d" % ci, BF16)
        k_ = A(S // 2, "ak%d" % ci, BF16)
        dma("sp", q_.ap[0:nr, :], qsrc[r0:r0 + nr, :], reads=E.q_slots, writes=[q_])
        dma("sp", k_.ap[0:nr, :], ksrc[r0:r0 + nr, :], reads=E.k_slots, writes=[k_])
        qs.append((q_, nr))
        ks.append((k_, nr))
    v_ = A(S // 2, "av", BF16)
    v3 = v_.ap.rearrange("p (t d) -> p t d", t=NT)
    for t4 in range(0, NT, 4):
        dma("sp", v3[:, t4:t4 + 4, :], SC[vname][t4 * 128:(t4 + 4) * 128, h * 128:(h + 1) * 128].rearrange("(t p) d -> p t d", p=128),
            reads=dsl(vname, t4, t4 + 4), writes=[v_])
    pT = [A(256, "pT%d" % i, BF16) for i in range(3)]
    rl = [A(512, "rl%d" % i) for i in range(2)]
    oo = [A(256, "oo%d" % i, BF16) for i in range(2)]
    bt = [A(NT, "bt%d" % i) for i in range(2)] if bias_fn else None
    pairs = [(J, i) for J in range(NB) for i in range(4 * J + 4)]
    st = {}

    def stage_a(idx):
        J, i = pairs[idx]
        lo = max(0, i - 4 * J)
        n0 = lo * 128
        ps = PS[4 + idx % 3]
        p_ = pT[idx % 3]
        for ci in range(len(qs)):
            (q_, nr), (k_, _) = qs[ci], ks[ci]
            mm(ps, ps.ap[:, n0:512], k_.ap[0:nr, i * 128:(i + 1) * 128], q_.ap[0:nr, J * 512 + n0:(J + 1) * 512],
               ci == 0, ci == len(qs) - 1, [q_, k_])
        if bias_fn is None:
            act(p_.ap[:, n0:512], ps.ap[:, n0:512], AF.Exp, [ps], [p_])
        else:
            b_ = bt[idx % 2]
            bias_fn(b_, i)
            for jj in range(lo, 4):
                act(p_.ap[:, jj * 128:(jj + 1) * 128], ps.ap[:, jj * 128:(jj + 1) * 128], AF.Exp, [ps, b_], [p_],
                    bias=b_.ap[:, 4 * J + jj:4 * J + jj + 1])
        if i >= 4 * J:
            tt(p_.ap[:, n0:n0 + 128], p_.ap[:, n0:n0 + 128], E.maskIb, ALU.mult, [p_, E.cb], [p_], eng="pool")

    def stage_b(idx):
        J, i = pairs[idx]
        lo = max(0, i - 4 * J)
        n0 = lo * 128
        last = 4 * J + 3
        p_ = pT[idx % 3]
        po, pl = PS[(J % 2) * 2], PS[(J % 2) * 2 + 1]
        mm(po, po.ap[:, n0:512], v3[:, i, :], p_.ap[:, n0:512], i == 0, i == last, [v_, p_])
        mm(pl, pl.ap[:, n0:512], E.onesb, p_.ap[:, n0:512], i == 0, i == last, [E.cb, p_])
        if i == last:
            r_, o_ = rl[J % 2], oo[J % 2]
            recip(r_.ap, pl.ap, [pl], [r_])
            tt(o_.ap, po.ap, r_.ap, ALU.mult, [po, r_], [o_])
            dma("pool", SC["br"][br_row0:br_row0 + 128, J * 512:(J + 1) * 512], o_.ap, reads=[o_], writes=dsl("br", J * 4, J * 4 + 4))

    LAG = int(os.environ.get('ALAG', '2'))
    for idx in range(len(pairs) + LAG):
        if idx < len(pairs):
            stage_a(idx)
        if idx >= LAG:
            stage_b(idx - LAG)
    P.release(m0)


def phase_mla(nc, P, l, s, env):
    E = Env(env)
    W, SC, PS, NT, NB, S = E.W, E.SC, E.PS, E.NT, E.NB, E.S
    op, dma, mm, tr, act, tt, ts, stt, cp, recip, dsl, psb = E.op, E.dma, E.mm, E.tr, E.act, E.tt, E.ts, E.stt, E.cp, E.recip, E.dsl, E.psb
    cst, cb = E.cst, E.cb
    A = P.alloc
    wuq = A(2 * 768 // 2, "wuq", BF16)
    wuq3 = wuq.ap.rearrange("p (k n) -> p k n", k=2)
    wukv = A(1024 // 2, "wukv", BF16)
    stg = [A(2 * 768, "mstg%d" % i) for i in range(2)]
    E.load_cast(wuq, wuq3, W["mla_w_uq"][l].rearrange("(k p) n -> p k n", p=128), 2 * 768, stg)
    E.load_cast(wukv, wukv.ap, W["mla_w_ukv"][l], 1024, stg)
    wv = A(256, "wv", BF16)
    for h in range(4):
        cp(wv.ap[:, h * 128:(h + 1) * 128], wukv.ap[:, h * 256 + 128:h * 256 + 256], [wukv], [wv], eng="pool")
    gq = A(8, "gqk")
    with nc.allow_non_contiguous_dma(reason="cols"):
        dma("sp", gq.ap[:, 0:1], E.col(W["mla_qk_q_g"][l, 0:128]), writes=[gq])
        dma("sp", gq.ap[0:64, 1:2], E.col(W["mla_qk_q_g"][l, 128:192]), writes=[gq])
        dma("sp", gq.ap[:, 2:3], E.col(W["mla_qk_k_g"][l, 0:128]), writes=[gq])
        dma("sp", gq.ap[0:64, 3:4], E.col(W["mla_qk_k_g"][l, 128:192]), writes=[gq])
    sc = 192.0 ** -0.5
    ts(gq.ap[:, 0:1], gq.ap[:, 0:1], sc, ALU.mult, [gq], [gq])
    ts(gq.ap[0:64, 1:2], gq.ap[0:64, 1:2], sc, ALU.mult, [gq], [gq])

    def b2(words, name, dt=None, n=2):
        return [A(words, "%s%d" % (name, i), dt) for i in range(n)]
    cqb, ckvb, krb = b2(512, "cqb", BF16), b2(256, "ckvb", BF16), b2(256, "krb", BF16)
    posi = A(512, "posi", I32)
    ang = A(512, "ang")
    cos2, sin2 = A(512, "cos2"), A(512, "sin2")
    sq1, sq2, sq3 = b2(256, "msq1", BF16), b2(256, "msq2", BF16), b2(256, "msq3", BF16)
    rs, rt = b2(512, "mrs"), b2(512, "mrt")
    ykr, rk, t1, t2 = A(512, "ykr"), A(512, "rk"), b2(512, "mt1"), b2(512, "mt2")
    ykrb = A(256, "ykrb", BF16)
    yq, yqb = b2(512, "yq"), b2(256, "yqb", BF16)
    o1, o2, o3 = b2(256, "mo1", BF16, 3), b2(256, "mo2", BF16, 3), b2(256, "mo3", BF16, 3)
    PI = math.pi
    mpi = A(8, "mpi")
    op("pool", lambda e: e.memset(mpi.ap, -PI), writes=[mpi])
    cnt = 0
    for b in range(NB):
        bs = slice(b * 512, (b + 1) * 512)
        cq_, ckv_, kr_ = cqb[b % 2], ckvb[b % 2], krb[b % 2]
        cq3 = cq_.ap.rearrange("p (k t) -> p k t", k=2)
        dma("sp", cq3, SC["cq"][:, bs].rearrange("(k p) t -> p k t", p=128), reads=dsl("cq", b * 4, b * 4 + 4), writes=[cq_])
        dma("sp", ckv_.ap, SC["ckv"][:, bs], reads=dsl("ckv", b * 4, b * 4 + 4), writes=[ckv_])
        dma("sp", kr_.ap[0:64, :], SC["kr"][:, bs], reads=dsl("kr", b * 4, b * 4 + 4), writes=[kr_])
        dma("sp", posi.ap[0:64, :], E.pos_in[s, bs].partition_broadcast(64), writes=[posi])
        cp(ang.ap[0:64, :], posi.ap[0:64, :], [posi], [ang])
        ts(ang.ap[0:64, :], ang.ap[0:64, :], cst.ap[0:64, C_INVF:C_INVF + 1], ALU.mult, [ang, cst], [ang])
        for tab, phi in ((cos2, 0.5 * PI), (sin2, 0.0)):
            ta_ = tab.ap[0:64, :]
            ts(ta_, ang.ap[0:64, :], 1.0 / (2 * PI), ALU.mult, [ang], [tab], s2=phi / (2 * PI) + 0.5, alu2=ALU.add)
            cp(posi.ap[0:64, :], ta_, [tab], [posi])
            cp(t1[0].ap[0:64, :], posi.ap[0:64, :], [posi], [t1[0]])
            tt(ta_, ta_, t1[0].ap[0:64, :], ALU.subtract, [tab, t1[0]], [tab])
            ts(t1[0].ap[0:64, :], ta_, 0.0, ALU.is_lt, [tab], [t1[0]])
            tt(ta_, ta_, t1[0].ap[0:64, :], ALU.add, [tab, t1[0]], [tab])
            act(ta_, ta_, AF.Sin, [tab, mpi], [tab], bias=mpi.ap[0:64, 0:1], scale=2 * PI)
        ksq = sq3[b % 2]
        act(ksq.ap[0:64, :], kr_.ap[0:64, :], AF.Square, [kr_], [ksq])
        ts(ykr.ap[0:64, :], kr_.ap[0:64, :], gq.ap[0:64, 3:4], ALU.mult, [kr_, gq], [ykr])
        cp(ykrb.ap[0:64, :], ykr.ap[0:64, :], [ykr], [ykrb], eng="pool")
        pr = PS[0]
        mm(pr, pr.ap[0:64, :], E.rotb, ykrb.ap[0:64, :], True, True, [cb, ykrb])
        tt(rk.ap[0:64, :], ykr.ap[0:64, :], cos2.ap[0:64, :], ALU.mult, [ykr, cos2], [rk])
        tt(t1[0].ap[0:64, :], pr.ap[0:64, :], sin2.ap[0:64, :], ALU.mult, [pr, sin2], [t1[0]])
        tt(rk.ap[0:64, :], rk.ap[0:64, :], t1[0].ap[0:64, :], ALU.add, [rk, t1[0]], [rk])
        for tq in range(4):
            t = b * 4 + tq
            pv = PS[1]
            mm(pv, pv.ap, ckv_.ap[:, tq * 128:(tq + 1) * 128], wv.ap, True, True, [ckv_, wv])
            o_ = o3[t % 3]
            act(o_.ap, pv.ap, AF.Identity, [pv], [o_])
            dma("pool", SC["mv"][t * 128:(t + 1) * 128, :], o_.ap, reads=[o_], writes=dsl("mv", t))
        for h in range(4):
            c2 = cnt % 2
            cnt += 1
            pqn, pqr, pkn, pss = PS[2], PS[3], PS[4], PS[5 + c2]
            for kc in range(2):
                mm(pqn, pqn.ap, wuq3[:, kc, h * 192:h * 192 + 128], cq3[:, kc, :], kc == 0, kc == 1, [wuq, cq_])
            for kc in range(2):
                mm(pqr, pqr.ap[0:64, :], wuq3[:, kc, h * 192 + 128:h * 192 + 192], cq3[:, kc, :], kc == 0, kc == 1, [wuq, cq_])
            mm(pkn, pkn.ap, wukv.ap[:, h * 256:h * 256 + 128], ckv_.ap, True, True, [wukv, ckv_])
            s1, s2 = sq1[c2], sq2[c2]
            act(s1.ap, pqn.ap, AF.Square, [pqn], [s1])
            act(s2.ap[0:64, :], pqr.ap[0:64, :], AF.Square, [pqr], [s2])
            mm(pss, pss.ap, E.onesb, s1.ap, True, False, [cb, s1])
            mm(pss, pss.ap, E.onesb[0:64, :], s2.ap[0:64, :], False, True, [cb, s2])
            E.rstd_from(rs[c2], rs[c2].ap, pss.ap, 192.0, [pss], rt[c2], rt[c2].ap)
            oq = o1[cnt % 3]
            stt(oq.ap, pqn.ap, gq.ap[:, 0:1], rs[c2].ap, ALU.mult, ALU.mult, [pqn, gq, rs[c2]], [oq])
            dma("pool", SC["mq"][h * 192:h * 192 + 128, bs], oq.ap, reads=[oq], writes=dsl("mq", b * 4, b * 4 + 4))
            y_, yb_ = yq[c2], yqb[c2]
            stt(y_.ap[0:64, :], pqr.ap[0:64, :], gq.ap[0:64, 1:2], rs[c2].ap[0:64, :], ALU.mult, ALU.mult, [pqr, gq, rs[c2]], [y_])
            cp(yb_.ap[0:64, :], y_.ap[0:64, :], [y_], [yb_], eng="pool")
            pr2 = PS[7]
            mm(pr2, pr2.ap[0:64, :], E.rotb, yb_.ap[0:64, :], True, True, [cb, yb_])
            ta, tb = t1[1], t2[c2]
            tt(ta.ap[0:64, :], y_.ap[0:64, :], cos2.ap[0:64, :], ALU.mult, [y_, cos2], [ta])
            tt(tb.ap[0:64, :], pr2.ap[0:64, :], sin2.ap[0:64, :], ALU.mult, [pr2, sin2], [tb])
            oq2 = o2[cnt % 3]
            tt(oq2.ap[0:64, :], ta.ap[0:64, :], tb.ap[0:64, :], ALU.add, [ta, tb], [oq2])
            dma("pool", SC["mq"][h * 192 + 128:h * 192 + 192, bs], oq2.ap[0:64, :], reads=[oq2], writes=dsl("mq", b * 4, b * 4 + 4))
            s1k = sq1[c2]
            act(s1k.ap, pkn.ap, AF.Square, [pkn], [s1k])
            pss2 = PS[1]
            mm(pss2, pss2.ap, E.onesb, s1k.ap, True, False, [cb, s1k])
            mm(pss2, pss2.ap, E.onesb[0:64, :], ksq.ap[0:64, :], False, True, [cb, ksq])
            E.rstd_from(rs[c2], rs[c2].ap, pss2.ap, 192.0, [pss2], rt[c2], rt[c2].ap)
            ok = o1[(cnt + 1) % 3]
            stt(ok.ap, pkn.ap, gq.ap[:, 2:3], rs[c2].ap, ALU.mult, ALU.mult, [pkn, gq, rs[c2]], [ok])
            dma("pool", SC["mk"][h * 192:h * 192 + 128, bs], ok.ap, reads=[ok], writes=dsl("mk", b * 4, b * 4 + 4))
            ok2 = o2[(cnt + 1) % 3]
            tt(ok2.ap[0:64, :], rk.ap[0:64, :], rs[c2].ap[0:64, :], ALU.mult, [rk, rs[c2]], [ok2])
            dma("pool", SC["mk"][h * 192 + 128:h * 192 + 192, bs], ok2.ap[0:64, :], reads=[ok2], writes=dsl("mk", b * 4, b * 4 + 4))
    P.release(E.base_mark)
    E.q_slots = dsl("mq", 0, NT)
    E.k_slots = dsl("mk", 0, NT)
    for h in range(4):
        attention(nc, P, E, SC["mq"][h * 192:(h + 1) * 192, :], SC["mk"][h * 192:(h + 1) * 192, :], "mv", h,
                  [(0, 128), (128, 64)], None, 512 + h * 128)


def phase_sg(nc, P, l, s, env):
    E = Env(env)
    W, SC, PS, NT = E.W, E.SC, E.PS, E.NT
    op, dma, mm, tr, act, tt, ts, stt, cp, dsl = E.op, E.dma, E.mm, E.tr, E.act, E.tt, E.ts, E.stt, E.cp, E.dsl
    A = P.alloc
    wsf_ = A(512, "sgwf")
    wsT = A(256, "sgwT", BF16)
    bbc = A(512, "sgb")
    dma("sp", wsf_.ap.rearrange("p (g s) -> p g s", g=4), W["sg_w_s"][l].rearrange("g t s -> t g s"), writes=[wsf_])
    dma("sp", bbc.ap, W["sg_b_s"][l].rearrange("g t -> (g t)").partition_broadcast(128), writes=[bbc])
    wsb_ = A(256, "sgwb", BF16)
    cp(wsb_.ap, wsf_.ap, [wsf_], [wsb_])
    pt = PS[0]
    ptb_ = E.psb(0)
    for g in range(4):
        tr(pt, ptb_[:, g * 128:(g + 1) * 128], wsb_.ap[:, g * 128:(g + 1) * 128], E.identb, [wsb_, E.cb])
    wsT0 = A(256, "sgwT0", BF16)
    cp(wsT0.ap, ptb_[:, 0:512], [pt], [wsT0], eng="act")
    tt(wsT.ap, wsT0.ap, E.mask4Ib.ap, ALU.mult, [wsT0, E.mask4Ib], [wsT], eng="pool")
    vb = [A(256, "sgvb%d" % i, BF16) for i in range(2)]
    ub = [A(256, "sgub%d" % i, BF16) for i in range(2)]
    tf = [A(512, "sgtf%d" % i) for i in range(2)]
    ob = [A(256, "sgob%d" % i, BF16) for i in range(2)]
    for t in range(NT):
        i2 = t % 2
        sl = slice(t * 128, (t + 1) * 128)
        dma("sp", vb[i2].ap, SC["sgv"][sl, :], reads=dsl("sgv", t), writes=[vb[i2]])
        dma("sp", ub[i2].ap.rearrange("p (g t) -> p g t", g=4), SC["sgu"][:, sl].rearrange("(g p) t -> p g t", p=128),
            reads=dsl("sgu", t), writes=[ub[i2]])
        ps = PS[1 + i2]
        for g in range(4):
            gs_ = slice(g * 128, (g + 1) * 128)
            mm(ps, ps.ap[:, gs_], vb[i2].ap[:, gs_], wsT.ap[:, gs_], True, True, [vb[i2], wsT])
        tt(tf[i2].ap, ps.ap, bbc.ap, ALU.add, [ps, bbc], [tf[i2]])
        tt(ob[i2].ap, tf[i2].ap, ub[i2].ap, ALU.mult, [tf[i2], ub[i2]], [ob[i2]], eng="pool")
        dma("pool", SC["br"][1024:1536, sl].rearrange("(g p) t -> p g t", p=128), ob[i2].ap.rearrange("p (g t) -> p g t", g=4),
            reads=[ob[i2]], writes=dsl("br", t))


def phase_fox(nc, P, l, s, env):
    E = Env(env)
    W, SC, PS, NT = E.W, E.SC, E.PS, E.NT
    op, dma, mm, act, tt, ts, stt, cp, dsl = E.op, E.dma, E.mm, E.act, E.tt, E.ts, E.stt, E.cp, E.dsl
    A = P.alloc
    fb = A(8, "foxb")
    dma("sp", fb.ap[:, 0:4], W["fox_f_bias"][l].partition_broadcast(128), writes=[fb])
    Fcol = A(NT * 4, "Fcol")
    Fref = A(NT * 4, "Fref")
    carry = A(8, "carry")
    op("pool", lambda e: e.memset(carry.ap, 0.0), writes=[carry])
    gsb = [A(16, "fgs%d" % i) for i in range(2)]
    fpf = [A(24, "fpf%d" % i) for i in range(2)]
    fpb = [A(8, "fpb%d" % i, BF16) for i in range(2)]
    Fref3 = Fref.ap.rearrange("p (h t) -> p h t", h=4)
    for t in range(NT):
        g_ = gsb[t % 2]
        dma("sp", g_.ap[:, 0:12], SC["gsm"][t * 128:(t + 1) * 128, :], reads=dsl("gsm", t), writes=[g_])
        tt(g_.ap[:, 12:16], g_.ap[:, 8:12], fb.ap[:, 0:4], ALU.add, [g_, fb], [g_])
        act(g_.ap[:, 12:16], g_.ap[:, 12:16], AF.Exp, [g_], [g_], scale=-1.0)
        act(g_.ap[:, 12:16], g_.ap[:, 12:16], AF.Ln, [g_], [g_], bias=1.0)
        ts(g_.ap[:, 12:16], g_.ap[:, 12:16], -1.0, ALU.mult, [g_], [g_])
        pc, ptot = PS[6], PS[7]
        E.split_bf16(g_.ap[:, 12:16], g_, fpf[t % 2], fpb[t % 2], 3)
        for k in range(3):
            mm(pc, pc.ap[:, 0:4], E.maskIb, fpb[t % 2].ap[:, 4 * k:4 * k + 4], k == 0, k == 2, [E.cb, fpb[t % 2]])
        for k in range(3):
            mm(ptot, ptot.ap[:, 0:4], E.onesb, fpb[t % 2].ap[:, 4 * k:4 * k + 4], k == 0, k == 2, [E.cb, fpb[t % 2]])
        tt(Fcol.ap[:, t * 4:(t + 1) * 4], pc.ap[:, 0:4], carry.ap[:, 0:4], ALU.add, [pc, carry], [Fcol])
        tt(carry.ap[:, 0:4], carry.ap[:, 0:4], ptot.ap[:, 0:4], ALU.add, [carry, ptot], [carry])
        cp(Fref3[:, :, t], carry.ap[:, 0:4], [carry], [Fref])
    E.q_slots = dsl("fq", 0, NT)
    E.k_slots = dsl("fk", 0, NT)
    for h in range(4):
        def bias_fn(b_, i, h=h):
            ts(b_.ap[:, 0:NT], Fref3[:, h, :], Fcol.ap[:, i * 4 + h:i * 4 + h + 1], ALU.subtract, [Fref, Fcol], [b_])
        attention(nc, P, E, SC["fq"][h * 128:(h + 1) * 128, :], SC["fk"][h * 128:(h + 1) * 128, :], "fv", h,
                  [(0, 128)], bias_fn, 1536 + h * 128)


def phase_merge_ffn(nc, P, l, s, env):
    E = Env(env)
    W, SC, PS, NT, NB, S = E.W, E.SC, E.PS, E.NT, E.NB, E.S
    op, dma, mm, tr, act, tt, ts, stt, cp, dsl, psb = E.op, E.dma, E.mm, E.tr, E.act, E.tt, E.ts, E.stt, E.cp, E.dsl, E.psb
    A = P.alloc
    out, XS = E.out, E.XS
    x_src = E.x_in if l == 0 else out
    wbr = A(16 * 1024 // 2, "wbr", BF16)
    wbr3 = wbr.ap.rearrange("p (k n) -> p k n", k=16)
    wo = A(8 * 1024 // 2, "wo", BF16)
    wo3 = wo.ap.rearrange("p (k n) -> p k n", k=8)
    stg = [A(4096, "fstg%d" % i) for i in range(2)]
    for i in range(4):
        E.load_cast(wbr, wbr3[:, i * 4:(i + 1) * 4, :], W["w_branch"][l, i].rearrange("(k p) n -> p k n", p=128), 4096, stg)
    for hf in range(2):
        E.load_cast(wo, wo3[:, hf * 4:(hf + 1) * 4, :], W["w_out"][l, hf * 512:(hf + 1) * 512, :].rearrange("(k p) n -> p k n", p=128), 4096, stg)
    brb = [A(16 * 512 // 2, "brb%d" % i, BF16) for i in range(2)]
    gb = [A(4 * 512 // 2, "gb%d" % i, BF16) for i in range(2)]
    mf = [A(512, "mf%d" % i) for i in range(2)]
    tf = [A(512, "mtf%d" % i) for i in range(2)]
    mT = [A(8 * 512 // 2, "mT%d" % i, BF16) for i in range(2)]
    xt = [A(1024, "mx%d" % i) for i in range(2)]
    cnt = 0
    for b in range(NB):
        bs = slice(b * 512, (b + 1) * 512)
        br_ = brb[b % 2]
        br3 = br_.ap.rearrange("p (k t) -> p k t", k=16)
        for i4 in range(4):
            dma("sp", br3[:, i4 * 4:(i4 + 1) * 4, :], SC["br"][i4 * 512:(i4 + 1) * 512, bs].rearrange("(k p) t -> p k t", p=128),
                reads=dsl("br", b * 4, b * 4 + 4), writes=[br_])
        m_ = mT[b % 2]
        m3 = m_.ap.rearrange("p (k t) -> p k t", k=8)
        for c in range(8):
            g_ = gb[c % 2]
            g3 = g_.ap.rearrange("p (i t) -> p i t", i=4)
            dma("sp", g3, SC["gates"][:, bs].rearrange("(i c p) t -> c p i t", i=4, p=128)[c], reads=dsl("gates", b * 4, b * 4 + 4), writes=[g_])
            acc = mf[c % 2]
            for i in range(4):
                ps = PS[cnt % 4]
                cnt += 1
                for kc in range(4):
                    mm(ps, ps.ap, wbr3[:, i * 4 + kc, c * 128:(c + 1) * 128], br3[:, i * 4 + kc, :], kc == 0, kc == 3, [wbr, br_])
                if i == 0:
                    tt(acc.ap, ps.ap, g3[:, 0, :], ALU.mult, [ps, g_], [acc])
                else:
                    t_ = tf[i % 2]
                    tt(t_.ap, ps.ap, g3[:, i, :], ALU.mult, [ps, g_], [t_])
                    if i < 3:
                        tt(acc.ap, acc.ap, t_.ap, ALU.add, [acc, t_], [acc], eng="pool")
                    else:
                        tt(m3[:, c, :], acc.ap, t_.ap, ALU.add, [acc, t_], [m_], eng="pool")
        for tq in range(4):
            t = b * 4 + tq
            x_ = xt[t % 2]
            dma("sp", x_.ap, x_src[s, t * 128:(t + 1) * 128, :], reads=[XS[s][t]], writes=[x_])
            for hf in range(2):
                ps = PS[4 + (t * 2 + hf) % 4]
                for kc in range(8):
                    mm(ps, ps.ap, m3[:, kc, tq * 128:(tq + 1) * 128], wo3[:, kc, hf * 512:(hf + 1) * 512], kc == 0, kc == 7, [m_, wo])
                tt(x_.ap[:, hf * 512:(hf + 1) * 512], x_.ap[:, hf * 512:(hf + 1) * 512], ps.ap, ALU.add, [x_, ps], [x_])
            dma("pool", out[s, t * 128:(t + 1) * 128, :], x_.ap, reads=[x_], writes=[XS[s][t]])
    P.release(E.base_mark)
    P.pm('ffn')
    hT = A(8 * S // 2, "h2T", BF16)
    hT3 = hT.ap.rearrange("p (k t) -> p k t", k=8)
    hT_t = [Slot("h2T%d" % t) for t in range(NT)]
    for t_ in hT_t:
        t_.r = list(hT.r)
    m0_ = P.mark()
    g2bc = A(1024, "g2bc")
    dma("sp", g2bc.ap, W["norm2_g"][l].partition_broadcast(128), writes=[g2bc])
    xb = [A(1024, "fx%d" % i) for i in range(2)]
    hb = [A(512, "fh%d" % i, BF16) for i in range(2)]
    sm = [A(8, "fsm%d" % i) for i in range(2)]
    junk = A(512, "fjunk", BF16)
    for t in range(NT):
        x_, ht, st = xb[t % 2], hb[t % 2], sm[t % 2]
        dma("sp", x_.ap, out[s, t * 128:(t + 1) * 128, :], reads=[XS[s][t]], writes=[x_])
        act(junk.ap, x_.ap, AF.Square, [x_], [junk, st], accum=st.ap[:, 0:1])
        E.rstd_from(st, st.ap[:, 2:3], st.ap[:, 0:1], float(D), [st], st, st.ap[:, 1:2])
        stt(ht.ap, x_.ap, st.ap[:, 2:3], g2bc.ap, ALU.mult, ALU.mult, [x_, st, g2bc], [ht])
        pb, pbv = PS[t % 2], psb(t % 2)
        for k in range(8):
            tr(pb, pbv[:, k * 128:(k + 1) * 128], ht.ap[:, k * 128:(k + 1) * 128], E.identb, [ht, E.cb])
        c0 = t * 128
        op("act", lambda e, pbv=pbv, c0=c0: e.activation(out=hT3[:, :, c0:c0 + 128], in_=pbv.rearrange("p (k t) -> p k t", k=8),
                                                         func=AF.Identity), reads=[pb], writes=[hT_t[t]])
    P.release(m0_)
    m1 = P.mark()
    for half in range(2):
        P.release(m1)
        w1 = A(8 * 2048 // 2, "w1", BF16)
        w13 = w1.ap.rearrange("p (k n) -> p k n", k=8)
        w2 = A(16 * 1024 // 2, "w2", BF16)
        w23 = w2.ap.rearrange("p (k n) -> p k n", k=16)
        stg = [A(2048, "gstg%d" % i) for i in range(2)]
        for q in range(4):
            for kc2 in range(4):
                E.load_cast(w1, w13[:, kc2 * 2:kc2 * 2 + 2, q * 512:(q + 1) * 512],
                            W["w_ff1"][l, kc2 * 256:(kc2 + 1) * 256, half * 2048 + q * 512:half * 2048 + (q + 1) * 512].rearrange("(k p) n -> p k n", p=128),
                            1024, stg)
        for q in range(8):
            E.load_cast(w2, w23[:, q * 2:(q + 1) * 2, :],
                        W["w_ff2"][l, half * 2048 + q * 256:half * 2048 + (q + 1) * 256, :].rearrange("(k p) n -> p k n", p=128), 2048, stg)
        f1 = [A(16 * 512 // 2, "f1T%d" % i, BF16) for i in range(1)]
        rl = [A(512, "frl%d" % i) for i in range(2)]
        xo = [A(1024, "fxo%d" % i) for i in range(2)]
        cnt = 0
        for b in range(NB):
            f_ = f1[0]
            f3 = f_.ap.rearrange("p (k t) -> p k t", k=16)
            for fc in range(16):
                ps = PS[2 + cnt % 3]
                r_ = rl[cnt % 2]
                cnt += 1
                for kc in range(8):
                    mm(ps, ps.ap, w13[:, kc, fc * 128:(fc + 1) * 128], hT3[:, kc, b * 512:(b + 1) * 512], kc == 0, kc == 7,
                       [w1] + hT_t[b * 4:(b + 1) * 4])
                act(r_.ap, ps.ap, AF.Relu, [ps], [r_])
                tt(f3[:, fc, :], r_.ap, r_.ap, ALU.mult, [r_], [f_], eng=("dve" if fc % 2 else "pool"))
            for tq in range(4):
                t = b * 4 + tq
                x_ = xo[t % 2]
                dma("sp", x_.ap, out[s, t * 128:(t + 1) * 128, :], reads=[XS[s][t]], writes=[x_])
                for hf in range(2):
                    ps = PS[5 + (t * 2 + hf) % 3]
                    for kc in range(16):
                        mm(ps, ps.ap, f3[:, kc, tq * 128:(tq + 1) * 128], w23[:, kc, hf * 512:(hf + 1) * 512], kc == 0, kc == 15, [f_, w2])
                    tt(x_.ap[:, hf * 512:(hf + 1) * 512], x_.ap[:, hf * 512:(hf + 1) * 512], ps.ap, ALU.add, [x_, ps], [x_])
                dma("pool", out[s, t * 128:(t + 1) * 128, :], x_.ap, reads=[x_], writes=[XS[s][t]])
    fold(hT, hT_t)


_CACHE = {}


def kernel(**inputs):
    NCORE = 8
    x = np.asarray(inputs["x"], np.float32)
    B, S, _ = x.shape
    NSEQ = B // NCORE
    depth = int(np.asarray(inputs["w_in"]).shape[0])
    key = (NSEQ, S, depth)
    if key not in _CACHE:
        _CACHE[key] = build(NSEQ, S, depth)
    nc = _CACHE[key]
    consts = make_consts()
    shared = {k: np.ascontiguousarray(np.asarray(v, np.float32)) for k, v in inputs.items() if k not in ("x", "positions")}
    pos = np.asarray(inputs["positions"]).astype(np.int32)
    in_maps = []
    for c in range(NCORE):
        m = dict(shared)
        m["x"] = np.ascontiguousarray(x[c * NSEQ:(c + 1) * NSEQ])
        m["positions"] = np.ascontiguousarray(pos[c * NSEQ:(c + 1) * NSEQ])
        m["consts"] = consts
        in_maps.append(m)
    res = run_bass_kernel_spmd(nc, in_maps, core_ids=list(range(NCORE)))
    return np.concatenate([np.asarray(r["out"], np.float32) for r in res.results], axis=0)
```
